# Optimizing a Trainium2 kernel written in Bass

```python
import jax, jax.numpy as jnp
from jax import lax
import numpy as np

D_MODEL = 2048
BATCH = 2
SEQ = 16384
DEPTH = 4
DEC_BATCH = 8
DEC_SEQ = 16
PAST_LEN = 1024

CHUNK = 64
HEAD_DIM = 128
A_HEADS = 8
A_PREV_CHUNKS = 8
A_REL_MAX = 128
A_REL_SIZE = (CHUNK - 1) + A_REL_MAX + 1
B_HEADS = 8
B_KV_HEADS = 2
B_WINDOW = 128
B_PREV_CHUNKS = B_WINDOW // CHUNK
M_HEADS = 4
M_HEAD_DIM = 256
N_MEM = 256
D_FF = 5632

A_WIDTH = A_HEADS * HEAD_DIM
B_Q_WIDTH = B_HEADS * HEAD_DIM
B_KV_WIDTH = B_KV_HEADS * HEAD_DIM
M_WIDTH = M_HEADS * M_HEAD_DIM
N_BRANCH = 3
IN_SPLITS = (A_WIDTH, A_WIDTH, A_WIDTH, B_Q_WIDTH, B_KV_WIDTH, B_KV_WIDTH, M_WIDTH, N_BRANCH * D_MODEL)
IN_WIDTH = sum(IN_SPLITS)
DEEPNORM_ALPHA = (2 * DEPTH) ** 0.25
DEEPNORM_BETA = (8 * DEPTH) ** -0.25
LN_EPS = 1e-5
NEG_INF = -1e30

kernel_name = 'hybrid_streaming_encoder_step'


def _layernorm(x, g, b):
    xf = x.astype(jnp.float32)
    mu = jnp.mean(xf, axis=-1, keepdims=True)
    var = jnp.mean(jnp.square(xf - mu), axis=-1, keepdims=True)
    y = (xf - mu) * lax.rsqrt(var + LN_EPS) * g.astype(jnp.float32) + b.astype(jnp.float32)
    return y.astype(x.dtype)


def _post_norm(x, f, g, b):
    return _layernorm(DEEPNORM_ALPHA * x + f, g, b)


def _swiglu(x, w_gu, w_down):
    gate, up = jnp.split(x @ w_gu, 2, axis=-1)
    return (jax.nn.silu(gate) * up) @ w_down


def _project(x, w_in):
    b, s, _ = x.shape
    cuts = np.cumsum(IN_SPLITS)[:-1].tolist()
    qa, ka, va, qb, kb, vb, qm, gates = jnp.split(x @ w_in, cuts, axis=-1)
    hd = (b, s, -1, HEAD_DIM)
    return (qa.reshape(hd), ka.reshape(hd), va.reshape(hd),
            qb.reshape(hd), kb.reshape(hd), vb.reshape(hd),
            qm.reshape(b, s, M_HEADS, M_HEAD_DIM), gates.reshape(b, s, N_BRANCH, D_MODEL))


def _rel_dist(offset, n_q, n_k):
    return offset + jnp.arange(n_q)[:, None] - jnp.arange(n_k)[None, :]


def _rel_bias_a(table, rel):
    idx = jnp.clip(rel, -(CHUNK - 1), A_REL_MAX) + (CHUNK - 1)
    return jnp.take(table.astype(jnp.float32), idx, axis=1)


def _alibi_bias(rel):
    slopes = 2.0 ** (-8.0 * (jnp.arange(B_HEADS, dtype=jnp.float32) + 1.0) / B_HEADS)
    return -slopes[:, None, None] * jnp.abs(rel).astype(jnp.float32)


def _to_chunks(x):
    b, s, h, d = x.shape
    return x.reshape(b, s // CHUNK, CHUNK, h, d)


def _gather_band(x, n_prev):
    b, s, g, d = x.shape
    nc = s // CHUNK
    xp = jnp.pad(x.reshape(b, nc, CHUNK, g, d), ((0, 0), (n_prev, 0), (0, 0), (0, 0), (0, 0)))
    return jnp.concatenate([xp[:, j:j + nc] for j in range(n_prev + 1)], axis=2)


def _band_valid(nc, n_prev):
    key_chunk = jnp.arange(nc)[:, None] - n_prev + (jnp.arange((n_prev + 1) * CHUNK) // CHUNK)[None, :]
    return key_chunk >= 0


def _band_attend(q, k, v, bias, valid, sink):
    b, n, nq, h, d = q.shape
    g = k.shape[3]
    r = h // g
    qg = q.reshape(b, n, nq, g, r, d)
    s = jnp.einsum('bnqgrd,bnkgd->bngrqk', qg, k, preferred_element_type=jnp.float32) * (d ** -0.5)
    s = s + bias.reshape(g, r, nq, -1)
    s = jnp.where(valid[None, :, None, None, None, :], s, NEG_INF)
    if sink is None:
        p = jax.nn.softmax(s, axis=-1)
    else:
        sk = sink.astype(jnp.float32).reshape(1, 1, g, r, 1, 1)
        m = jnp.maximum(jnp.max(s, axis=-1, keepdims=True), sk)
        e = jnp.exp(s - m)
        p = e / (jnp.sum(e, axis=-1, keepdims=True) + jnp.exp(sk - m))
    o = jnp.einsum('bngrqk,bnkgd->bnqgrd', p.astype(v.dtype), v)
    return o.reshape(b, n * nq, h * d)


def _mem_kv(mem, w_mem_kv):
    b, n, _ = mem.shape
    mk, mv = jnp.split(mem @ w_mem_kv, 2, axis=-1)
    return mk.reshape(b, n, M_HEADS, M_HEAD_DIM), mv.reshape(b, n, M_HEADS, M_HEAD_DIM)


def _mem_attend(q, mk, mv):
    b, s = q.shape[:2]
    sc = jnp.einsum('bshd,bmhd->bhsm', q, mk.astype(q.dtype), preferred_element_type=jnp.float32) * (M_HEAD_DIM ** -0.5)
    p = jax.nn.softmax(sc, axis=-1).astype(q.dtype)
    return jnp.einsum('bhsm,bmhd->bshd', p, mv.astype(q.dtype)).reshape(b, s, M_WIDTH)


def _merge(gates, oa, ob, om, w_br_a, w_br_b, w_br_m, w_out):
    g = jax.nn.sigmoid(gates.astype(jnp.float32)).astype(oa.dtype)
    h = g[:, :, 0] * (oa @ w_br_a) + g[:, :, 1] * (ob @ w_br_b) + g[:, :, 2] * (om @ w_br_m)
    return h @ w_out


def _mix_prompt(x, mem, w_in, w_br_a, w_br_b, w_br_m, w_out, w_mem_kv, rel_tab, sink):
    b, s, _ = x.shape
    nc = s // CHUNK
    qa, ka, va, qb, kb, vb, qm, gates = _project(x, w_in)
    la = (A_PREV_CHUNKS + 1) * CHUNK
    lb = (B_PREV_CHUNKS + 1) * CHUNK
    oa = _band_attend(_to_chunks(qa), _gather_band(ka, A_PREV_CHUNKS), _gather_band(va, A_PREV_CHUNKS),
                      _rel_bias_a(rel_tab, _rel_dist(A_PREV_CHUNKS * CHUNK, CHUNK, la)),
                      _band_valid(nc, A_PREV_CHUNKS), None)
    ob = _band_attend(_to_chunks(qb), _gather_band(kb, B_PREV_CHUNKS), _gather_band(vb, B_PREV_CHUNKS),
                      _alibi_bias(_rel_dist(B_PREV_CHUNKS * CHUNK, CHUNK, lb)),
                      _band_valid(nc, B_PREV_CHUNKS), sink)
    mk, mv = _mem_kv(mem, w_mem_kv)
    om = _mem_attend(qm, mk, mv)
    y = _merge(gates, oa, ob, om, w_br_a, w_br_b, w_br_m, w_out)
    a_keep = min(A_PREV_CHUNKS * CHUNK, s)
    b_keep = min(B_PREV_CHUNKS * CHUNK, s)
    return y, ka[:, s - a_keep:], va[:, s - a_keep:], kb[:, s - b_keep:], vb[:, s - b_keep:], mk, mv


def _mix_sample(x, ca_k, ca_v, cb_k, cb_v, cm_k, cm_v, w_in, w_br_a, w_br_b, w_br_m, w_out, rel_tab, sink):
    b, t, _ = x.shape
    qa, ka, va, qb, kb, vb, qm, gates = _project(x, w_in)
    wa = ca_k.shape[1]
    wb = cb_k.shape[1]
    ka_all = jnp.concatenate([ca_k.astype(ka.dtype), ka], axis=1)[:, None]
    va_all = jnp.concatenate([ca_v.astype(va.dtype), va], axis=1)[:, None]
    kb_all = jnp.concatenate([cb_k.astype(kb.dtype), kb], axis=1)[:, None]
    vb_all = jnp.concatenate([cb_v.astype(vb.dtype), vb], axis=1)[:, None]
    oa = _band_attend(qa[:, None], ka_all, va_all, _rel_bias_a(rel_tab, _rel_dist(wa, t, wa + t)),
                      jnp.ones((1, wa + t), dtype=bool), None)
    ob = _band_attend(qb[:, None], kb_all, vb_all, _alibi_bias(_rel_dist(wb, t, wb + t)),
                      jnp.ones((1, wb + t), dtype=bool), sink)
    om = _mem_attend(qm, cm_k, cm_v)
    y = _merge(gates, oa, ob, om, w_br_a, w_br_b, w_br_m, w_out)
    return y, ka, va, kb, vb


def setup_inputs(seed: int = 0) -> dict:
    key = jax.random.key(seed)
    ks = jax.random.split(key, 23)
    f32 = jnp.float32

    def nrm(k, shape, scale):
        return jax.random.normal(k, shape, f32) * scale

    a_win = min(A_PREV_CHUNKS * CHUNK, PAST_LEN)
    b_win = min(B_PREV_CHUNKS * CHUNK, PAST_LEN)
    return {
        'x_prompt': nrm(ks[0], (BATCH, SEQ, D_MODEL), 1.0),
        'x_sample': nrm(ks[1], (DEC_BATCH, DEC_SEQ, D_MODEL), 1.0),
        'cache_a_k': nrm(ks[2], (DEPTH, DEC_BATCH, a_win, A_HEADS, HEAD_DIM), 1.0),
        'cache_a_v': nrm(ks[3], (DEPTH, DEC_BATCH, a_win, A_HEADS, HEAD_DIM), 1.0),
        'cache_b_k': nrm(ks[4], (DEPTH, DEC_BATCH, b_win, B_KV_HEADS, HEAD_DIM), 1.0),
        'cache_b_v': nrm(ks[5], (DEPTH, DEC_BATCH, b_win, B_KV_HEADS, HEAD_DIM), 1.0),
        'cache_mem_k': nrm(ks[6], (DEPTH, DEC_BATCH, N_MEM, M_HEADS, M_HEAD_DIM), 1.0),
        'cache_mem_v': nrm(ks[7], (DEPTH, DEC_BATCH, N_MEM, M_HEADS, M_HEAD_DIM), 1.0),
        'mem_prompt': nrm(ks[8], (BATCH, N_MEM, D_MODEL), 1.0),
        'w_in': nrm(ks[9], (DEPTH, D_MODEL, IN_WIDTH), D_MODEL ** -0.5),
        'w_br_a': nrm(ks[10], (DEPTH, A_WIDTH, D_MODEL), A_WIDTH ** -0.5),
        'w_br_b': nrm(ks[11], (DEPTH, B_Q_WIDTH, D_MODEL), B_Q_WIDTH ** -0.5),
        'w_br_m': nrm(ks[12], (DEPTH, M_WIDTH, D_MODEL), M_WIDTH ** -0.5),
        'w_out': nrm(ks[13], (DEPTH, D_MODEL, D_MODEL), DEEPNORM_BETA * D_MODEL ** -0.5),
        'w_mem_kv': nrm(ks[14], (DEPTH, D_MODEL, 2 * M_WIDTH), D_MODEL ** -0.5),
        'rel_bias_a': nrm(ks[15], (DEPTH, A_HEADS, A_REL_SIZE), 0.3),
        'sink_b': nrm(ks[16], (DEPTH, B_HEADS), 0.5),
        'ffn1_gu': nrm(ks[17], (DEPTH, D_MODEL, 2 * D_FF), D_MODEL ** -0.5),
        'ffn1_down': nrm(ks[18], (DEPTH, D_FF, D_MODEL), DEEPNORM_BETA * D_FF ** -0.5),
        'ffn2_gu': nrm(ks[19], (DEPTH, D_MODEL, 2 * D_FF), D_MODEL ** -0.5),
        'ffn2_down': nrm(ks[20], (DEPTH, D_FF, D_MODEL), DEEPNORM_BETA * D_FF ** -0.5),
        'ln_g': 1.0 + nrm(ks[21], (DEPTH, 3, D_MODEL), 0.02),
        'ln_b': nrm(ks[22], (DEPTH, 3, D_MODEL), 0.02),
    }


def reference(x_prompt, x_sample, cache_a_k, cache_a_v, cache_b_k, cache_b_v, cache_mem_k, cache_mem_v,
              mem_prompt, w_in, w_br_a, w_br_b, w_br_m, w_out, w_mem_kv, rel_bias_a, sink_b,
              ffn1_gu, ffn1_down, ffn2_gu, ffn2_down, ln_g, ln_b):
    xp, xs = x_prompt, x_sample
    akp, avp, bkp, bvp, mkp, mvp = [], [], [], [], [], []
    aks, avs, bks, bvs = [], [], [], []
    for l in range(DEPTH):
        xp = _post_norm(xp, 0.5 * _swiglu(xp, ffn1_gu[l], ffn1_down[l]), ln_g[l, 0], ln_b[l, 0])
        mix, ak, av, bk, bv, mk, mv = _mix_prompt(xp, mem_prompt, w_in[l], w_br_a[l], w_br_b[l], w_br_m[l],
                                                  w_out[l], w_mem_kv[l], rel_bias_a[l], sink_b[l])
        xp = _post_norm(xp, mix, ln_g[l, 1], ln_b[l, 1])
        xp = _post_norm(xp, 0.5 * _swiglu(xp, ffn2_gu[l], ffn2_down[l]), ln_g[l, 2], ln_b[l, 2])
        akp.append(ak); avp.append(av); bkp.append(bk); bvp.append(bv); mkp.append(mk); mvp.append(mv)
        xs = _post_norm(xs, 0.5 * _swiglu(xs, ffn1_gu[l], ffn1_down[l]), ln_g[l, 0], ln_b[l, 0])
        mix, ak, av, bk, bv = _mix_sample(xs, cache_a_k[l], cache_a_v[l], cache_b_k[l], cache_b_v[l],
                                          cache_mem_k[l], cache_mem_v[l], w_in[l], w_br_a[l], w_br_b[l],
                                          w_br_m[l], w_out[l], rel_bias_a[l], sink_b[l])
        xs = _post_norm(xs, mix, ln_g[l, 1], ln_b[l, 1])
        xs = _post_norm(xs, 0.5 * _swiglu(xs, ffn2_gu[l], ffn2_down[l]), ln_g[l, 2], ln_b[l, 2])
        aks.append(ak); avs.append(av); bks.append(bk); bvs.append(bv)
    return (xp, xs,
            jnp.stack(akp), jnp.stack(avp), jnp.stack(bkp), jnp.stack(bvp), jnp.stack(mkp), jnp.stack(mvp),
            jnp.stack(aks), jnp.stack(avs), jnp.stack(bks), jnp.stack(bvs))
```

```python
import contextlib
import os
import numpy as np
import concourse.bass as bass
import concourse.mybir as mybir
from concourse.bass_utils import run_bass_kernel_spmd

F32 = mybir.dt.float32
BF16 = mybir.dt.bfloat16
AF = mybir.ActivationFunctionType
ALU = mybir.AluOpType

NL = 4
DM = 2048
DFF = 5632
SEQ = 16384
NTILE = 12
ZCOL = 12
ALPHA = (2 * NL) ** 0.25
EPS2 = 1e-5 / (ALPHA * ALPHA)
SCALE = 128.0 ** -0.5
MSCALE = 256.0 ** -0.5
NEG = -30000.0
NSLOT = 4

ENGS = ("pe", "act", "dve", "pool", "sp")
EPOCH = 30000
NEPOCH = 5
DMASETS = {("sp", "main"): 12, ("pool", "main"): 10, ("pool", "cast"): 4}


class Res:
    __slots__ = ("w", "r", "excl")

    def __init__(self, excl=False):
        self.w = None
        self.r = []
        self.excl = excl


def mkres(n):
    return [Res() for _ in range(n)]


class Ring:
    def __init__(self, n, base=0):
        self.n, self.i, self.base = n, 0, base

    def next(self):
        k = self.i
        self.i = (k + 1) % self.n
        return self.base + k


class Prog:
    def __init__(self, nc):
        self.nc = nc
        self.ops = {e: [] for e in ENGS}
        self.nsig = {e: 0 for e in ENGS}
        self.waited = {e: {} for e in ENGS}
        self.dma_rr = {k: 0 for k in DMASETS}
        self.dma_val = {}
        self.esem = {}
        self.dsem = {}

    def alloc_sems(self, stack):
        for e in ("pe", "act", "dve", "pool"):
            self.esem[e] = [stack.enter_context(self.nc.semaphore(f"s_{e}{i}")) for i in range(NEPOCH)]
        for (q, s), n in DMASETS.items():
            for k in range(n):
                self.dsem[(q, s, k)] = stack.enter_context(self.nc.semaphore(f"d_{q}_{s}{k}"))

    @staticmethod
    def _deps(reads, writes):
        deps = []
        for r in reads:
            if r.w is not None:
                deps.append(r.w)
            if r.excl:
                deps.extend(r.r)
        for w in writes:
            if w.w is not None:
                deps.append(w.w)
            deps.extend(w.r)
        return deps

    def _waits(self, eng, deps):
        need = {}
        for d in deps:
            if d[0] == "e":
                if d[1] == eng and eng == "pe":
                    continue
                key = ("e", d[1])
            else:
                key = ("d", d[1])
            if need.get(key, 0) < d[2]:
                need[key] = d[2]
        out = []
        wd = self.waited[eng]
        for key, val in need.items():
            if wd.get(key, 0) >= val:
                continue
            wd[key] = val
            out.append((key, val))
        return out

    @staticmethod
    def _commit(ev, reads, writes):
        for r in reads:
            r.r.append(ev)
        for w in writes:
            w.w = ev
            w.r = []

    def op(self, eng, fn, reads=(), writes=()):
        waits = self._waits(eng, self._deps(reads, writes))
        self.nsig[eng] += 1
        ev = ("e", eng, self.nsig[eng])
        self.ops[eng].append((waits, fn, ev))
        self._commit(ev, reads, writes)
        return ev

    def dma(self, q, out_ap, in_ap, reads=(), writes=(), semset="main"):
        deps = self._deps(reads, writes)
        k = self.dma_rr[(q, semset)]
        self.dma_rr[(q, semset)] = (k + 1) % DMASETS[(q, semset)]
        key = (q, semset, k)
        prev = self.dma_val.get(key, 0)
        if prev > 0:
            deps.append(("d", key, prev))
        waits = self._waits(q, deps)
        val = prev + 16
        self.dma_val[key] = val
        ev = ("d", key, val)
        self.ops[q].append((waits, (lambda e, o=out_ap, i=in_ap: e.dma_start(out=o, in_=i)), ev))
        self._commit(ev, reads, writes)
        return ev

    def finish(self):
        for q in ("sp", "pool"):
            deps = [("d", key, val) for key, val in self.dma_val.items() if key[0] == q]
            self.ops[q].append((self._waits(q, deps), None, None))

    def replay(self, eng, h):
        esem, dsem = self.esem, self.dsem
        for waits, fn, ev in self.ops[eng]:
            for key, val in waits:
                if key[0] == "e":
                    h.wait_ge(esem[key[1]][(val - 1) // EPOCH], (val - 1) % EPOCH + 1)
                else:
                    h.wait_ge(dsem[key[1]], val)
            if fn is None:
                continue
            ins = fn(h)
            if ev[0] == "e":
                ins.then_inc(esem[eng][(ev[2] - 1) // EPOCH], 1)
            else:
                ins.then_inc(dsem[ev[1]], 16)

    def run_block(self):
        self.finish()
        with self.nc.Block() as block:
            @block.tensor
            def _(h):
                self.replay("pe", h)

            @block.scalar
            def _(h):
                self.replay("act", h)

            @block.vector
            def _(h):
                self.replay("dve", h)

            @block.gpsimd
            def _(h):
                self.replay("pool", h)

            @block.sync
            def _(h):
                self.replay("sp", h)


def gen_slabs(get=None):
    def sl(src, k0, kc, c0, n):
        return None if get is None else get(src)[k0 * 128:(k0 + kc) * 128, c0:c0 + n]

    def ffn(w):
        for j in range(22):
            yield f"g{w}G{j}", 16, 256, sl(f"ffn{w}_gu", 0, 16, 256 * j, 256)
            yield f"g{w}U{j}", 16, 256, sl(f"ffn{w}_gu", 0, 16, DFF + 256 * j, 256)
        for mo in range(16):
            for hf in range(2):
                yield f"d{w}_{mo}_{hf}", 22, 128, sl(f"ffn{w}_down", 22 * hf, 22, 128 * mo, 128)

    yield from ffn(1)
    for j in range(4):
        yield f"ka{j}", 16, 256, sl("w_in", 0, 16, 1024 + 256 * j, 256)
    for j in range(4):
        yield f"va{j}", 16, 256, sl("w_in", 0, 16, 2048 + 256 * j, 256)
    yield "kb", 16, 256, sl("w_in", 0, 16, 4096, 256)
    yield "vb", 16, 256, sl("w_in", 0, 16, 4352, 256)
    for j in range(4):
        yield f"qa{j}", 16, 256, sl("w_in", 0, 16, 256 * j, 256)
    for j in range(4):
        yield f"qb{j}", 16, 256, sl("w_in", 0, 16, 3072 + 256 * j, 256)
    for j in range(4):
        yield f"qm{j}", 16, 256, sl("w_in", 0, 16, 4608 + 256 * j, 256)
    for m in range(16):
        ga = sl("w_in", 0, 16, 5632 + 128 * m, 128)
        gb = sl("w_in", 0, 16, 5632 + 2048 + 128 * m, 128)
        yield f"gab{m}", 16, 256, (None if get is None else np.concatenate([ga, gb], axis=1))
        yield f"gm{m}", 16, 128, sl("w_in", 0, 16, 5632 + 4096 + 128 * m, 128)
        ba = sl("w_br_a", 0, 8, 128 * m, 128)
        bb = sl("w_br_b", 0, 8, 128 * m, 128)
        bm = sl("w_br_m", 0, 8, 128 * m, 128)
        yield f"br{m}", 24, 128, (None if get is None else np.concatenate([ba, bb, bm], axis=0))
    for j in range(8):
        yield f"wo{j}", 16, 256, sl("w_out", 0, 16, 256 * j, 256)
    yield from ffn(2)
    for j in range(4):
        yield f"mk{j}", 16, 256, sl("w_mem_kv", 0, 16, 256 * j, 256)
    for j in range(4):
        yield f"mv{j}", 16, 256, sl("w_mem_kv", 0, 16, 1024 + 256 * j, 256)


def slab_layout():
    SL = {}
    off = 0
    pieces = []
    pstart = 0
    for name, KC, C, _ in gen_slabs():
        n = KC * C
        if off + n - pstart > 49152:
            pieces.append((pstart, off))
            pstart = off
        SL[name] = (off, KC, C, len(pieces))
        off += n
    pieces.append((pstart, off))
    return SL, off, pieces


def host_weight_stream(inputs, l):
    get = lambda name: inputs[name][l]
    parts = []
    for name, KC, C, d in gen_slabs(get):
        parts.append(np.ascontiguousarray(d.reshape(KC, 128, C).transpose(1, 0, 2)).reshape(128, KC * C))
    return np.ascontiguousarray(np.concatenate(parts, axis=1), dtype=np.float32)


def build_program(nlayers=NL, ti0=0):
    DBG = os.environ.get('KDBG', '').split(',')
    STAGE = int(os.environ.get('KSTAGE', '99'))
    nc = bass.Bass("TRN2", target_bir_lowering=False)
    SL, E, PIECES = slab_layout()
    NP = len(PIECES)

    def din(name, shape):
        return nc.dram_tensor(name, shape, F32, kind="ExternalInput")

    def dout(name, shape):
        return nc.dram_tensor(name, shape, F32, kind="ExternalOutput")

    xin_d = din("xin", [NTILE, 128, 16, 512]).ap()
    xs_d = din("xs", [128, 16, 16]).ap()
    kmask_d = din("kmask", [128, 16]).ap()
    WS = [din(f"ws{l}", [128, E]).ap() for l in range(nlayers)]
    lnp_d = din("lnp", [128, NL * 3 * 2 * 16]).ap()
    extA_t = din("extA", [NL, 8, 768])
    maskA_d = din("maskA", [128, 640]).ap()
    biasB_d = din("biasB", [8, 128, 256]).ap()
    sink_t = din("sink", [NL * 8])
    memT_d = din("memT", [128, 16, 256]).ap()
    cakT_d = din("cakT", [NL, 128, 8, 512]).ap()
    cav_d = din("cav", [NL, 512, 1024]).ap()
    cbkT_d = din("cbkT", [NL, 128, 2, 128]).ap()
    cbv_d = din("cbv", [NL, 128, 256]).ap()
    cmkT_d = din("cmkT", [NL, 128, 8, 256]).ap()
    cmv_d = din("cmv", [NL, 256, 1024]).ap()

    y_out = dout("y_out", [8, 128, 16, 512]).ap()
    ys_out = dout("ys_out", [128, 16, 16]).ap()
    akp = dout("akp", [NL, 128, 8, 512]).ap()
    avp = dout("avp", [NL, 512, 1024]).ap()
    bkp = dout("bkp", [NL, 128, 2, 128]).ap()
    bvp = dout("bvp", [NL, 128, 256]).ap()
    mkp = dout("mkp", [NL, 128, 8, 256]).ap()
    mvp = dout("mvp", [NL, 256, 1024]).ap()
    aks = dout("aks", [NL, 128, 8, 16]).ap()
    avs = dout("avs", [NL, 16, 1024]).ap()
    bks = dout("bks", [NL, 128, 2, 16]).ap()
    bvs = dout("bvs", [NL, 16, 256]).ap()

    WSb = [nc.dram_tensor(f"wsb{l}", [128, E], BF16, kind="Internal").ap() for l in range(nlayers)]
    xres_d = nc.dram_tensor("xres", [NTILE, 128, 16, 512], F32, kind="Internal").ap()
    xres_s = nc.dram_tensor("xress", [128, 16, 16], F32, kind="Internal").ap()
    Dtoe_t = [nc.dram_tensor(f"dtoe{l}", [8, 128, 768], F32, kind="Internal") for l in range(nlayers)]
    biasA_d = nc.dram_tensor("biasA", [NL, 8, 128, 640], F32, kind="Internal").ap()

    P = Prog(nc)
    with contextlib.ExitStack() as st:
        P.alloc_sems(st)

        def sb(name, shape, dtype):
            return st.enter_context(nc.sbuf_tensor(name, shape, dtype))

        x32 = sb("x32", [128, 16, 512], F32)
        xb = sb("xb", [128, 16, 512], BF16)
        U = sb("U", [128, 44 * 512], BF16)
        KA = sb("KA", [128, 2, 8, 512], BF16)
        VA = sb("VA", [128, 2, 4, 1024], BF16)
        KB = sb("KB", [128, 2, 2, 512], BF16)
        VB = sb("VB", [128, 2, 4, 256], BF16)
        wring = sb("wring", [128, NSLOT, 4096], BF16)
        bAr = sb("bAr", [128, 2, 640], F32)
        bBr = sb("bBr", [128, 2, 256], F32)
        memKT = sb("memKT", [128, 8, 256], BF16)
        memV = sb("memV", [128, 2, 1024], BF16)
        actT = sb("actT", [128, 3, 512], F32)
        dveT = sb("dveT", [128, 4, 512], F32)
        PT = sb("PT", [128, 4, 512], BF16)
        lnT = sb("lnT", [128, 3, 512], F32)
        ones32 = sb("ones32", [128, 128], F32)
        onesb = sb("onesb", [128, 128], BF16)
        lnp = sb("lnp_s", [128, NL * 3 * 2 * 16], F32)
        kmask = sb("kmask_s", [128, 16], F32)
        esink = sb("esink", [128, NL * 8], F32)
        epsc = sb("epsc", [128, 1], F32)
        ps = st.enter_context(nc.psum_tensor("ps", [128, 8, 512], F32))

        r_x32, r_xb, r_u = mkres(16), mkres(16), mkres(44)
        r_KA = [mkres(8), mkres(8)]
        r_VA = [mkres(4), mkres(4)]
        r_KB = [mkres(2), mkres(2)]
        r_VB = [mkres(4), mkres(4)]
        r_w, r_bA, r_bB = mkres(NSLOT), mkres(2), mkres(2)
        r_memKT, r_memV = mkres(8), mkres(2)
        r_actT, r_dveT, r_PT, r_ln = mkres(3), mkres(4), mkres(4), mkres(3)
        r_ps = [Res(excl=True) for _ in range(8)]
        r_const = Res()
        r_wsb = [mkres(NP) for _ in range(nlayers)]
        r_xres = mkres(NTILE)
        r_xres_s = Res()
        r_dtoe = mkres(nlayers)
        r_biasA = [mkres(8) for _ in range(nlayers)]

        bankR, wR, actR, dveR, ptR = Ring(8), Ring(NSLOT), Ring(3), Ring(4), Ring(4)
        bAR, bBR, sR = Ring(2), Ring(2), Ring(4)
        evtog = [0]

        def mm(bank, pairs, reads, M=128, N=512):
            out = ps[0:M, bank, 0:N]
            n = len(pairs)

            def fn(e, out=out, pairs=pairs, n=n):
                ins = None
                for i, (lt, rh) in enumerate(pairs):
                    ins = e.matmul(out, lt, rh, start=(i == 0), stop=(i == n - 1))
                return ins
            P.op("pe", fn, reads=reads, writes=[r_ps[bank]])

        def mm1(out, lt, rh, start, stop, reads, bank):
            P.op("pe", lambda e, o=out, a=lt, b=rh, s=start, t=stop: e.matmul(o, a, b, start=s, stop=t),
                 reads=reads, writes=[r_ps[bank]])

        def evac(out, in_, reads, writes, eng=None):
            if eng is None:
                eng = "act" if evtog[0] == 0 else "dve"
                evtog[0] ^= 1
            if eng == "act":
                P.op("act", lambda e, o=out, i=in_: e.activation(out=o, in_=i, func=AF.Copy), reads, writes)
            else:
                P.op("dve", lambda e, o=out, i=in_: e.tensor_copy(out=o, in_=i), reads, writes)

        def out_f32(dram_ap, psum_ap, p, n, rbank):
            k = dveR.next()
            evac(dveT[0:p, k, 0:n], psum_ap, [rbank], [r_dveT[k]])
            P.dma("pool", dram_ap, dveT[0:p, k, 0:n], reads=[r_dveT[k]])

        def ld(l, name):
            off, KC, C, piece = SL[name]
            k = wR.next()
            P.dma("sp", wring[:, k, 0:KC * C], WSb[l][:, off:off + KC * C], reads=[r_wsb[l][piece]], writes=[r_w[k]])
            return wring[:, k, 0:KC * C].rearrange("p (kc c) -> p kc c", c=C), r_w[k]

        def uch(idx, T, lo=0):
            return U[:, idx * 512 + lo: idx * 512 + T]

        def ffn(l, w, T, mid_hook=None):
            for j in range(22):
                G, rG = ld(l, f"g{w}G{j}")
                Uw, rU = ld(l, f"g{w}U{j}")
                bg = [bankR.next(), bankR.next()]
                bu = [bankR.next(), bankR.next()]
                if j == 0:
                    for kc in range(16):
                        def fn(e, kc=kc, G=G, Uw=Uw, bg=bg, bu=bu):
                            ins = None
                            for c in range(2):
                                ins = e.matmul(ps[:, bg[c], 0:T], G[:, kc, c * 128:(c + 1) * 128], xb[:, kc, 0:T], start=(kc == 0), stop=(kc == 15))
                            for c in range(2):
                                ins = e.matmul(ps[:, bu[c], 0:T], Uw[:, kc, c * 128:(c + 1) * 128], xb[:, kc, 0:T], start=(kc == 0), stop=(kc == 15))
                            return ins
                        P.op("pe", fn, [rG, rU, r_xb[kc]], [r_ps[bg[0]], r_ps[bg[1]], r_ps[bu[0]], r_ps[bu[1]]])
                else:
                    for c in range(2):
                        mm(bg[c], [(G[:, kc, c * 128:(c + 1) * 128], xb[:, kc, 0:T]) for kc in range(16)], [rG] + r_xb, N=T)
                    for c in range(2):
                        mm(bu[c], [(Uw[:, kc, c * 128:(c + 1) * 128], xb[:, kc, 0:T]) for kc in range(16)], [rU] + r_xb, N=T)
                for c in range(2):
                    m = 2 * j + c
                    t = actR.next()
                    P.op("act", lambda e, o=actT[:, t, 0:T], i=ps[:, bg[c], 0:T]: e.activation(out=o, in_=i, func=AF.Silu),
                         [r_ps[bg[c]]], [r_actT[t]])
                    P.op("dve", lambda e, o=uch(m, T), a=ps[:, bu[c], 0:T], b=actT[:, t, 0:T]:
                         e.tensor_tensor(out=o, in0=a, in1=b, op=ALU.mult),
                         [r_ps[bu[c]], r_actT[t]], [r_u[m]])
            if STAGE < 2:
                return
            if mid_hook is not None:
                mid_hook()
            for mo in range(16):
                D0, r0 = ld(l, f"d{w}_{mo}_0")
                D1, r1 = ld(l, f"d{w}_{mo}_1")
                b = bankR.next()
                pairs = [(D0[:, kc, :], uch(kc, T)) for kc in range(22)] + [(D1[:, kc, :], uch(22 + kc, T)) for kc in range(22)]
                mm(b, pairs, [r0, r1] + r_u, N=T)
                P.op("dve", lambda e, o=x32[:, mo, 0:T], a=ps[:, b, 0:T]:
                     e.scalar_tensor_tensor(out=o, in0=a, scalar=0.5 / ALPHA, in1=o, op0=ALU.mult, op1=ALU.add),
                     [r_ps[b], r_x32[mo]], [r_x32[mo]])
                accum_stats(mo, T)

        def accum_stats(mo, T):
            s1, s2, xc = lnT[:, 0, 0:T], lnT[:, 2, 0:T], x32[:, mo, 0:T]
            if mo == 0:
                P.op("dve", lambda e: e.tensor_copy(out=s1, in_=xc), [r_x32[mo]], [r_ln[0]])
                P.op("act", lambda e: e.activation(out=s2, in_=xc, func=AF.Square), [r_x32[mo]], [r_ln[2]])
            else:
                t = actR.next()
                sq = actT[:, t, 0:T]
                P.op("act", lambda e: e.activation(out=sq, in_=xc, func=AF.Square), [r_x32[mo]], [r_actT[t]])
                P.op("dve", lambda e: e.tensor_tensor(out=s1, in0=s1, in1=xc, op=ALU.add), [r_x32[mo], r_ln[0]], [r_ln[0]])
                P.op("dve", lambda e: e.tensor_tensor(out=s2, in0=s2, in1=sq, op=ALU.add), [r_actT[t], r_ln[2]], [r_ln[2]])

        def layernorm(l, i, T, write_xb=True):
            b1, b2 = bankR.next(), bankR.next()
            mm1(ps[:, b1, 0:T], ones32[:], lnT[:, 0, 0:T], True, True, [r_ln[0], r_const], b1)
            mm1(ps[:, b2, 0:T], ones32[:], lnT[:, 2, 0:T], True, True, [r_ln[2], r_const], b2)
            mean, rstd, nmr = lnT[:, 0, 0:T], lnT[:, 1, 0:T], lnT[:, 2, 0:T]
            P.op("dve", lambda e: e.tensor_scalar(out=mean, in0=ps[:, b1, 0:T], scalar1=1.0 / DM, scalar2=None, op0=ALU.mult),
                 [r_ps[b1]], [r_ln[0]])
            P.op("dve", lambda e: e.tensor_tensor(out=nmr, in0=mean, in1=mean, op=ALU.mult), [r_ln[0]], [r_ln[2]])
            P.op("dve", lambda e: e.scalar_tensor_tensor(out=nmr, in0=ps[:, b2, 0:T], scalar=1.0 / DM, in1=nmr,
                                                         op0=ALU.mult, op1=ALU.subtract), [r_ps[b2], r_ln[2]], [r_ln[2]])
            P.op("act", lambda e: e.activation(out=nmr, in_=nmr, func=AF.Sqrt, bias=epsc[:, 0:1], scale=1.0),
                 [r_ln[2], r_const], [r_ln[2]])
            P.op("dve", lambda e: e.reciprocal(out=rstd, in_=nmr), [r_ln[2]], [r_ln[1]])
            P.op("dve", lambda e: e.scalar_tensor_tensor(out=nmr, in0=mean, scalar=-1.0, in1=rstd, op0=ALU.mult, op1=ALU.mult),
                 [r_ln[0], r_ln[1]], [r_ln[2]])
            base = (l * 3 + i) * 32
            for mo in range(16):
                xc = x32[:, mo, 0:T]
                g_ap = lnp[:, base + mo: base + mo + 1]
                b_ap = lnp[:, base + 16 + mo: base + 16 + mo + 1]
                P.op("dve", lambda e, xc=xc: e.tensor_tensor(out=xc, in0=xc, in1=rstd, op=ALU.mult), [r_x32[mo], r_ln[1]], [r_x32[mo]])
                P.op("dve", lambda e, xc=xc: e.tensor_tensor(out=xc, in0=xc, in1=nmr, op=ALU.add), [r_x32[mo], r_ln[2]], [r_x32[mo]])
                if write_xb:
                    P.op("act", lambda e, xc=xc, o=xb[:, mo, 0:T], g=g_ap, b=b_ap: e.activation(out=o, in_=xc, func=AF.Identity, bias=b, scale=g),
                         [r_x32[mo], r_const], [r_xb[mo]])
                P.op("act", lambda e, xc=xc, g=g_ap, b=b_ap: e.activation(out=xc, in_=xc, func=AF.Identity, bias=b, scale=g),
                     [r_x32[mo], r_const], [r_x32[mo]])

        def proj_kv(l, T, half, outs):
            nb = max(1, T // 128)
            tw = min(T, 128)
            for j in range(4):
                W, r = ld(l, f"ka{j}")
                for c in range(2):
                    h = 2 * j + c
                    b = bankR.next()
                    mm(b, [(W[:, kc, c * 128:(c + 1) * 128], xb[:, kc, 0:T]) for kc in range(16)], [r] + r_xb, N=T)
                    evac(KA[:, half, h, 0:T], ps[:, b, 0:T], [r_ps[b]], [r_KA[half][h]])
                    if outs and "ak" in outs:
                        out_f32(outs["ak"][:, h, :], ps[:, b, 0:T], 128, T, r_ps[b])
            for j in range(4):
                W, r = ld(l, f"va{j}")
                for tb in range(nb):
                    b = bankR.next()
                    mm(b, [(xb[:, kc, tb * 128: tb * 128 + tw], W[:, kc, :]) for kc in range(16)], [r] + r_xb, M=tw, N=256)
                    evac(VA[0:tw, half, tb, j * 256:(j + 1) * 256], ps[0:tw, b, 0:256], [r_ps[b]], [r_VA[half][tb]])
                    if outs and "av" in outs:
                        out_f32(outs["av"][tb * 128: tb * 128 + tw, j * 256:(j + 1) * 256], ps[0:tw, b, 0:256], tw, 256, r_ps[b])
            W, r = ld(l, "kb")
            for g in range(2):
                b = bankR.next()
                mm(b, [(W[:, kc, g * 128:(g + 1) * 128], xb[:, kc, 0:T]) for kc in range(16)], [r] + r_xb, N=T)
                evac(KB[:, half, g, 0:T], ps[:, b, 0:T], [r_ps[b]], [r_KB[half][g]])
                if outs and "bk" in outs:
                    lo = max(0, T - 128)
                    out_f32(outs["bk"][:, g, :], ps[:, b, lo:T], 128, T - lo, r_ps[b])
            W, r = ld(l, "vb")
            for tb in range(nb):
                b = bankR.next()
                mm(b, [(xb[:, kc, tb * 128: tb * 128 + tw], W[:, kc, :]) for kc in range(16)], [r] + r_xb, M=tw, N=256)
                evac(VB[0:tw, half, tb, :], ps[0:tw, b, 0:256], [r_ps[b]], [r_VB[half][tb]])
                if outs and "bv" in outs and tb == nb - 1:
                    out_f32(outs["bv"], ps[0:tw, b, 0:256], tw, 256, r_ps[b])

        def proj_q(l, prefix, T):
            for j in range(4):
                W, r = ld(l, f"{prefix}{j}")
                for c in range(2):
                    idx = 2 * j + c
                    b = bankR.next()
                    mm(b, [(W[:, kc, c * 128:(c + 1) * 128], xb[:, kc, 0:T]) for kc in range(16)], [r] + r_xb, N=T)
                    evac(uch(24 + idx, T), ps[:, b, 0:T], [r_ps[b]], [r_u[24 + idx]])

        def attend(l, T, kind, blocks):
            nblk = len(blocks)
            items = [(h, bi) for h in range(8) for bi in range(nblk)]
            LA = 3
            state = {}

            def stage1(h, bi):
                if bi == 0:
                    if kind == "A":
                        s = bAR.next()
                        P.dma("sp", bAr[:, s, :], biasA_d[l, h], reads=[r_biasA[l][h]], writes=[r_bA[s]])
                        state[h] = (bAr[:, s, :], r_bA[s])
                    else:
                        s = bBR.next()
                        P.dma("sp", bBr[:, s, :], biasB_d[h], writes=[r_bB[s]])
                        state[h] = (bBr[:, s, :], r_bB[s])
                bias, rb = state[h]
                half, blk, nk, qlo, qhi, bc0, kmcol, _ = blocks[bi]
                nq = qhi - qlo
                bs = sR.next()
                if kind == "A":
                    kt, rk = KA[:, half, h, blk * 128: blk * 128 + nk], r_KA[half][h]
                else:
                    kt, rk = KB[:, half, h // 4, blk * 128: blk * 128 + nk], r_KB[half][h // 4]
                mm1(ps[0:nk, bs, 0:nq], kt, uch(24 + h, qhi, qlo), True, True, [rk, r_u[24 + h]], bs)
                d = dveR.next()
                P.op("dve", lambda e, o=dveT[0:nk, d, 0:nq], a=ps[0:nk, bs, 0:nq], b=bias[0:nk, bc0:bc0 + nq]:
                     e.scalar_tensor_tensor(out=o, in0=a, scalar=SCALE, in1=b, op0=ALU.mult, op1=ALU.add),
                     [r_ps[bs], rb], [r_dveT[d]])
                p = ptR.next()
                P.op("act", lambda e, o=PT[0:nk, p, 0:nq], a=dveT[0:nk, d, 0:nq], km=kmask[0:nk, kmcol:kmcol + 1]:
                     e.activation(out=o, in_=a, func=AF.Exp, bias=km, scale=1.0),
                     [r_dveT[d], r_const], [r_PT[p]])
                return p

            def stage2(h, bi, p):
                half, blk, nk, qlo, qhi, bc0, kmcol, stf = blocks[bi]
                nq = qhi - qlo
                bO, bD = 4 + (h % 2), 6 + (h % 2)
                last = bi == nblk - 1
                if kind == "A":
                    v, rv = VA[0:nk, half, blk, h * 128:(h + 1) * 128], r_VA[half][blk]
                else:
                    g = h // 4
                    v, rv = VB[0:nk, half, blk, g * 128:(g + 1) * 128], r_VB[half][blk]
                mm1(ps[:, bO, qlo:qhi], v, PT[0:nk, p, 0:nq], stf, last, [rv, r_PT[p]], bO)
                mm1(ps[:, bD, qlo:qhi], onesb[0:nk, :], PT[0:nk, p, 0:nq], stf, last, [r_PT[p], r_const], bD)
                if last:
                    d = dveR.next()
                    addv = 1e-30 if kind == "A" else esink[:, l * 8 + h: l * 8 + h + 1]
                    P.op("dve", lambda e, o=dveT[:, d, 0:T], a=ps[:, bD, 0:T], s=addv:
                         e.tensor_scalar(out=o, in0=a, scalar1=s, scalar2=None, op0=ALU.add),
                         [r_ps[bD], r_const], [r_dveT[d]])
                    P.op("dve", lambda e, o=dveT[:, d, 0:T]: e.reciprocal(out=o, in_=o), [r_dveT[d]], [r_dveT[d]])
                    oidx = h if kind == "A" else 8 + h
                    P.op("dve", lambda e, o=uch(oidx, T), a=ps[:, bO, 0:T], b=dveT[:, d, 0:T]:
                         e.tensor_tensor(out=o, in0=a, in1=b, op=ALU.mult),
                         [r_ps[bO], r_dveT[d]], [r_u[oidx]])

            pend = []
            for k in range(len(items) + LA):
                if k < len(items):
                    pend.append(stage1(*items[k]))
                if k >= LA:
                    stage2(items[k - LA][0], items[k - LA][1], pend[k - LA])

        def attend_m(l, T):
            for hm in range(4):
                pts = []
                for mb in range(2):
                    bs = sR.next()
                    prs = [(memKT[:, hm * 2 + hh, mb * 128:(mb + 1) * 128], uch(24 + hm * 2 + hh, T)) for hh in range(2)]
                    mm(bs, prs, [r_memKT[hm * 2], r_memKT[hm * 2 + 1], r_u[24 + hm * 2], r_u[24 + hm * 2 + 1]], N=T)
                    p = ptR.next()
                    P.op("act", lambda e, o=PT[:, p, 0:T], a=ps[:, bs, 0:T]: e.activation(out=o, in_=a, func=AF.Exp, scale=MSCALE),
                         [r_ps[bs]], [r_PT[p]])
                    pts.append(p)
                bD = 6 + hm % 2
                rpt = [r_PT[pts[0]], r_PT[pts[1]]]
                for dhh in range(2):
                    c0 = hm * 256 + dhh * 128
                    mm(4 + dhh, [(memV[:, mb, c0:c0 + 128], PT[:, pts[mb], 0:T]) for mb in range(2)], r_memV + rpt, N=T)
                mm(bD, [(onesb[:], PT[:, pts[mb], 0:T]) for mb in range(2)], rpt + [r_const], N=T)
                d = dveR.next()
                P.op("dve", lambda e, o=dveT[:, d, 0:T], a=ps[:, bD, 0:T]: e.reciprocal(out=o, in_=a), [r_ps[bD]], [r_dveT[d]])
                for dhh in range(2):
                    oidx = 16 + hm * 2 + dhh
                    P.op("dve", lambda e, o=uch(oidx, T), a=ps[:, 4 + dhh, 0:T], b=dveT[:, d, 0:T]:
                         e.tensor_tensor(out=o, in0=a, in1=b, op=ALU.mult),
                         [r_ps[4 + dhh], r_dveT[d]], [r_u[oidx]])

        def merge(l, T):
            for m in range(16):
                Wab, rab = ld(l, f"gab{m}")
                Wm, rm = ld(l, f"gm{m}")
                Wbr, rbr = ld(l, f"br{m}")
                bg = [bankR.next() for _ in range(3)]
                mm(bg[0], [(Wab[:, kc, 0:128], xb[:, kc, 0:T]) for kc in range(16)], [rab] + r_xb, N=T)
                mm(bg[1], [(Wab[:, kc, 128:256], xb[:, kc, 0:T]) for kc in range(16)], [rab] + r_xb, N=T)
                mm(bg[2], [(Wm[:, kc, :], xb[:, kc, 0:T]) for kc in range(16)], [rm] + r_xb, N=T)
                bb = [bankR.next() for _ in range(3)]
                for br in range(3):
                    mm(bb[br], [(Wbr[:, br * 8 + kc, :], uch(br * 8 + kc, T)) for kc in range(8)],
                       [rbr] + r_u[br * 8: br * 8 + 8], N=T)
                sg = []
                for br in range(3):
                    t = actR.next()
                    P.op("act", lambda e, o=actT[:, t, 0:T], a=ps[:, bg[br], 0:T]: e.activation(out=o, in_=a, func=AF.Sigmoid),
                         [r_ps[bg[br]]], [r_actT[t]])
                    sg.append(t)
                d1, d2 = dveR.next(), dveR.next()
                t1, t2 = dveT[:, d1, 0:T], dveT[:, d2, 0:T]
                P.op("dve", lambda e, t1=t1, a=ps[:, bb[0], 0:T], b=actT[:, sg[0], 0:T]: e.tensor_tensor(out=t1, in0=a, in1=b, op=ALU.mult),
                     [r_ps[bb[0]], r_actT[sg[0]]], [r_dveT[d1]])
                P.op("dve", lambda e, t2=t2, a=ps[:, bb[1], 0:T], b=actT[:, sg[1], 0:T]: e.tensor_tensor(out=t2, in0=a, in1=b, op=ALU.mult),
                     [r_ps[bb[1]], r_actT[sg[1]]], [r_dveT[d2]])
                P.op("dve", lambda e, t1=t1, t2=t2: e.tensor_tensor(out=t1, in0=t1, in1=t2, op=ALU.add), [r_dveT[d1], r_dveT[d2]], [r_dveT[d1]])
                P.op("dve", lambda e, t2=t2, a=ps[:, bb[2], 0:T], b=actT[:, sg[2], 0:T]: e.tensor_tensor(out=t2, in0=a, in1=b, op=ALU.mult),
                     [r_ps[bb[2]], r_actT[sg[2]], r_dveT[d2]], [r_dveT[d2]])
                P.op("dve", lambda e, t1=t1, t2=t2, o=uch(24 + m, T): e.tensor_tensor(out=o, in0=t1, in1=t2, op=ALU.add),
                     [r_dveT[d1], r_dveT[d2]], [r_u[24 + m]])

        def wout(l, T):
            for j in range(8):
                W, r = ld(l, f"wo{j}")
                for c in range(2):
                    mo = 2 * j + c
                    b = bankR.next()
                    mm(b, [(W[:, kc, c * 128:(c + 1) * 128], uch(24 + kc, T)) for kc in range(16)], [r] + r_u[24:40], N=T)
                    P.op("dve", lambda e, o=x32[:, mo, 0:T], a=ps[:, b, 0:T]:
                         e.scalar_tensor_tensor(out=o, in0=a, scalar=1.0 / ALPHA, in1=o, op0=ALU.mult, op1=ALU.add),
                         [r_ps[b], r_x32[mo]], [r_x32[mo]])
                    accum_stats(mo, T)

        def tile_layer(l, T, half, full, blocksA, blocksB, outs, mid_hook=None):
            ffn(l, 1, T)
            if STAGE < 3:
                return
            layernorm(l, 0, T)
            if STAGE < 4:
                return
            ko = os.environ.get('KOUT')
            if ko is not None and outs:
                outs = {k: v for k, v in outs.items() if k in ko.split(',')}
            proj_kv(l, T, half, outs if 'noouts' not in DBG else None)
            if not full:
                return
            proj_q(l, "qa", T)
            attend(l, T, "A", blocksA)
            proj_q(l, "qb", T)
            attend(l, T, "B", blocksB)
            proj_q(l, "qm", T)
            attend_m(l, T)
            merge(l, T)
            wout(l, T)
            layernorm(l, 1, T)
            ffn(l, 2, T, mid_hook)
            layernorm(l, 2, T, write_xb=False)

        def mem_kv(l):
            memT = U[:, 0:4096].rearrange("p (kc m) -> p kc m", m=256)
            P.dma("pool", memT, memT_d, writes=r_u[0:8])
            for j in range(4):
                W, r = ld(l, f"mk{j}")
                for c in range(2):
                    idx = 2 * j + c
                    b = bankR.next()
                    mm(b, [(W[:, kc, c * 128:(c + 1) * 128], memT[:, kc, :]) for kc in range(16)], [r] + r_u[0:8], N=256)
                    evac(memKT[:, idx, :], ps[:, b, 0:256], [r_ps[b]], [r_memKT[idx]])
                    out_f32(mkp[l, :, idx, :], ps[:, b, 0:256], 128, 256, r_ps[b])
            for j in range(4):
                W, r = ld(l, f"mv{j}")
                for mb in range(2):
                    b = bankR.next()
                    mm(b, [(memT[:, kc, mb * 128:(mb + 1) * 128], W[:, kc, :]) for kc in range(16)], [r] + r_u[0:8], N=256)
                    evac(memV[:, mb, j * 256:(j + 1) * 256], ps[:, b, 0:256], [r_ps[b]], [r_memV[mb]])
                    out_f32(mvp[l, mb * 128:(mb + 1) * 128, j * 256:(j + 1) * 256], ps[:, b, 0:256], 128, 256, r_ps[b])

        P.op("dve", lambda e: e.memset(ones32[:], 1.0), writes=[r_const])
        P.op("dve", lambda e: e.memset(onesb[:], 1.0), writes=[r_const])
        P.op("dve", lambda e: e.memset(epsc[:], EPS2), writes=[r_const])
        P.dma("sp", lnp[:], lnp_d, writes=[r_const])
        P.dma("sp", kmask[:], kmask_d, writes=[r_const])
        maskA = lnT[:, 0:2, :].rearrange("p a b -> p (a b)")[:, 0:640]
        P.dma("sp", maskA, maskA_d, writes=[r_ln[0], r_ln[1]])
        P.dma("sp", esink[:], bass.AP(sink_t, 0, [[0, 128], [1, NL * 8]]), writes=[r_const])
        P.op("act", lambda e: e.activation(out=esink[:], in_=esink[:], func=AF.Exp), [r_const], [r_const])

        def emit_cast(l, pi):
            a, b = PIECES[pi]
            P.dma("pool", WSb[l][:, a:b], WS[l][:, a:b], writes=[r_wsb[l][pi]], semset="cast")

        for pi in range(NP if 'nocast' not in DBG else 0):
            emit_cast(0, pi)

        for l in range(nlayers if 'nobias' not in DBG else 0):
            P.dma("sp", Dtoe_t[l].ap(), bass.AP(extA_t, l * 8 * 768, [[768, 8], [0, 128], [1, 768]]), writes=[r_dtoe[l]])
            for h in range(8):
                s = bAR.next()
                P.dma("sp", bAr[:, s, :], bass.AP(Dtoe_t[l], h * 128 * 768 + 127, [[767, 128], [1, 640]]),
                      reads=[r_dtoe[l]], writes=[r_bA[s]])
                P.op("dve", lambda e, o=bAr[:, s, :]: e.tensor_tensor(out=o, in0=o, in1=maskA, op=ALU.add),
                     [r_bA[s], r_ln[0], r_ln[1]], [r_bA[s]])
                P.dma("pool", biasA_d[l, h], bAr[:, s, :], reads=[r_bA[s]], writes=[r_biasA[l][h]])

        Z = ZCOL
        sblkA = [(0, 0, 128, 0, 16, 512, Z, True), (0, 1, 128, 0, 16, 384, Z, False), (0, 2, 128, 0, 16, 256, Z, False),
                 (0, 3, 128, 0, 16, 128, Z, False), (1, 0, 16, 0, 16, 0, Z, False)]
        sblkB = [(0, 3, 128, 0, 16, 128, Z, True), (1, 0, 16, 0, 16, 0, Z, False)]

        xb_prefetched = [False]
        for l in range(nlayers):
            lastl = l == nlayers - 1
            src = xs_d if l == 0 else xres_s
            P.dma("sp", x32[:, :, 0:16], src, reads=[r_xres_s], writes=r_x32)
            P.dma("pool", xb[:, :, 0:16], src, reads=[r_xres_s], writes=r_xb)
            if 'noscache' in DBG:
                break
            P.dma("pool", KA[:, 0, :, :], cakT_d[l], writes=r_KA[0])
            P.dma("pool", VA[:, 0, :, :], cav_d[l].rearrange("(blk p) c -> p blk c", p=128), writes=r_VA[0])
            P.dma("pool", KB[:, 0, :, 384:512], cbkT_d[l], writes=r_KB[0])
            P.dma("pool", VB[:, 0, 3, :], cbv_d[l], writes=[r_VB[0][3]])
            P.dma("pool", memKT[:], cmkT_d[l], writes=r_memKT)
            P.dma("pool", memV[:], cmv_d[l].rearrange("(mb p) c -> p mb c", p=128), writes=r_memV)
            souts = {"ak": aks[l], "av": avs[l], "bk": bks[l], "bv": bvs[l]}
            if 'nosample' not in DBG:
                tile_layer(l, 16, 1, True, sblkA, sblkB, souts)
            P.dma("pool", ys_out if lastl else xres_s, x32[:, :, 0:16], reads=r_x32, writes=[r_xres_s])
            if 'nomem' not in DBG:
                mem_kv(l)
            first = ti0 + l
            casts = [(l + 1, pi) for pi in range(NP)] if l + 1 < nlayers else []
            ntl = NTILE - first
            for ti in range(first, NTILE if 'noprompt' not in DBG else first):
                src = xin_d[ti] if l == 0 else xres_d[ti]
                P.dma("pool", x32[:], src, reads=[r_xres[ti]], writes=r_x32)
                if not xb_prefetched[0]:
                    P.dma("pool", xb[:], src, reads=[r_xres[ti]], writes=r_xb)
                xb_prefetched[0] = False
                full = ti >= first + 1

                def hook(ti=ti, l=l):
                    if ti + 1 < NTILE:
                        nsrc = xin_d[ti + 1] if l == 0 else xres_d[ti + 1]
                        P.dma("pool", xb[:], nsrc, reads=[r_xres[ti + 1]], writes=r_xb)
                        xb_prefetched[0] = True
                cur, prv = ti % 2, 1 - ti % 2
                bA = [(prv, 3, 128, 0, 512, 128, ti - 1, True), (prv, 0, 128, 0, 128, 512, ti - 1, False),
                      (prv, 1, 128, 0, 256, 384, ti - 1, False), (prv, 2, 128, 0, 384, 256, ti - 1, False),
                      (cur, 1, 128, 128, 512, 0, ti, False), (cur, 2, 128, 256, 512, 0, ti, False),
                      (cur, 3, 128, 384, 512, 0, ti, False), (cur, 0, 128, 0, 512, 0, ti, False)]
                bB = [(cur, 0, 128, 0, 256, 0, ti, True), (cur, 2, 128, 256, 512, 0, ti, False),
                      (prv, 3, 128, 0, 128, 128, ti - 1, False), (cur, 1, 128, 128, 384, 0, ti, False),
                      (cur, 3, 128, 384, 512, 0, ti, False)]
                pouts = {"ak": akp[l], "av": avp[l], "bk": bkp[l], "bv": bvp[l]} if ti == NTILE - 1 else None
                tile_layer(l, 512, cur, full, bA, bB, pouts, hook if full else None)
                if full:
                    if lastl:
                        if ti >= 4:
                            P.dma("pool", y_out[ti - 4], x32[:], reads=r_x32, writes=[r_xres[ti]])
                    else:
                        P.dma("pool", xres_d[ti], x32[:], reads=r_x32, writes=[r_xres[ti]])
                left = NTILE - ti
                ncast = (len(casts) + left - 1) // left
                for _ in range(ncast):
                    emit_cast(*casts.pop(0))
        P.run_block()
    return nc


def _const_masks():
    ki = np.arange(128)[:, None]
    qa = np.arange(640)[None, :]
    da = qa // 64 - ki // 64
    maskA = np.where((da >= 0) & (da <= 8), 0.0, NEG).astype(np.float32)
    qb = np.arange(256)[None, :]
    db = qb // 64 - ki // 64
    mb = np.where((db >= 0) & (db <= 2), 0.0, NEG).astype(np.float32)
    slopes = (2.0 ** (-8.0 * (np.arange(8, dtype=np.float32) + 1.0) / 8)).astype(np.float32)
    absrel = np.abs(qb - ki).astype(np.float32)
    biasB = (-slopes[:, None, None] * absrel[None] + mb[None]).astype(np.float32)
    return maskA, biasB


def prep_shared(inputs, nlayers=NL):
    sh = {}
    for l in range(nlayers):
        sh[f"ws{l}"] = host_weight_stream(inputs, l)
    g = inputs["ln_g"].reshape(NL, 3, 16, 128)
    b = inputs["ln_b"].reshape(NL, 3, 16, 128)
    lnp = np.stack([g, b], axis=2)
    sh["lnp"] = np.ascontiguousarray(lnp.transpose(4, 0, 1, 2, 3).reshape(128, NL * 3 * 2 * 16), dtype=np.float32)
    r = np.arange(768) - 127
    idx = np.clip(r, -63, 128) + 63
    sh["extA"] = np.ascontiguousarray(inputs["rel_bias_a"][:, :, idx], dtype=np.float32)
    maskA, biasB = _const_masks()
    sh["maskA"] = maskA
    sh["biasB"] = biasB
    sh["sink"] = np.ascontiguousarray(inputs["sink_b"].reshape(NL * 8), dtype=np.float32)
    return sh


def prep_core(inputs, c, sh):
    b, q = c // 4, c % 4
    m = dict(sh)
    xp = inputs["x_prompt"][b]
    xin = np.zeros((NTILE, 128, 16, 512), np.float32)
    kmask = np.zeros((128, 16), np.float32)
    for ti in range(NTILE):
        s0 = q * 4096 + (ti - 4) * 512
        if s0 < 0:
            kmask[:, ti] = NEG
            continue
        xin[ti] = xp[s0:s0 + 512].reshape(512, 16, 128).transpose(2, 1, 0)
    m["xin"] = xin
    m["kmask"] = kmask
    m["xs"] = np.ascontiguousarray(inputs["x_sample"][c].reshape(16, 16, 128).transpose(2, 1, 0))
    m["memT"] = np.ascontiguousarray(inputs["mem_prompt"][b].reshape(256, 16, 128).transpose(2, 1, 0))
    m["cakT"] = np.ascontiguousarray(inputs["cache_a_k"][:, c].transpose(0, 3, 2, 1))
    m["cav"] = np.ascontiguousarray(inputs["cache_a_v"][:, c].reshape(NL, 512, 1024))
    m["cbkT"] = np.ascontiguousarray(inputs["cache_b_k"][:, c].transpose(0, 3, 2, 1))
    m["cbv"] = np.ascontiguousarray(inputs["cache_b_v"][:, c].reshape(NL, 128, 256))
    ck = inputs["cache_mem_k"][:, c].reshape(NL, 256, 4, 2, 128)
    m["cmkT"] = np.ascontiguousarray(ck.transpose(0, 4, 2, 3, 1).reshape(NL, 128, 8, 256))
    m["cmv"] = np.ascontiguousarray(inputs["cache_mem_v"][:, c].reshape(NL, 256, 1024))
    return m


def assemble(res):
    y_prompt = np.zeros((2, SEQ, DM), np.float32)
    y_sample = np.zeros((8, 16, DM), np.float32)
    a_k_p = np.zeros((NL, 2, 512, 8, 128), np.float32)
    a_v_p = np.zeros((NL, 2, 512, 8, 128), np.float32)
    b_k_p = np.zeros((NL, 2, 128, 2, 128), np.float32)
    b_v_p = np.zeros((NL, 2, 128, 2, 128), np.float32)
    m_k_p = np.zeros((NL, 2, 256, 4, 256), np.float32)
    m_v_p = np.zeros((NL, 2, 256, 4, 256), np.float32)
    a_k_s = np.zeros((NL, 8, 16, 8, 128), np.float32)
    a_v_s = np.zeros((NL, 8, 16, 8, 128), np.float32)
    b_k_s = np.zeros((NL, 8, 16, 2, 128), np.float32)
    b_v_s = np.zeros((NL, 8, 16, 2, 128), np.float32)
    for c in range(8):
        r = res[c]
        b, q = c // 4, c % 4
        yo = r["y_out"]
        y_prompt[b, q * 4096:(q + 1) * 4096] = yo.transpose(0, 3, 2, 1).reshape(4096, DM)
        y_sample[c] = r["ys_out"].transpose(2, 1, 0).reshape(16, DM)
        a_k_s[:, c] = r["aks"].transpose(0, 3, 2, 1)
        a_v_s[:, c] = r["avs"].reshape(NL, 16, 8, 128)
        b_k_s[:, c] = r["bks"].transpose(0, 3, 2, 1)
        b_v_s[:, c] = r["bvs"].reshape(NL, 16, 2, 128)
        if q == 3:
            a_k_p[:, b] = r["akp"].transpose(0, 3, 2, 1)
            a_v_p[:, b] = r["avp"].reshape(NL, 512, 8, 128)
            b_k_p[:, b] = r["bkp"].transpose(0, 3, 2, 1)
            b_v_p[:, b] = r["bvp"].reshape(NL, 128, 2, 128)
        if q == 0:
            mk = r["mkp"].reshape(NL, 128, 4, 2, 256)
            m_k_p[:, b] = mk.transpose(0, 4, 2, 3, 1).reshape(NL, 256, 4, 256)
            m_v_p[:, b] = r["mvp"].reshape(NL, 256, 4, 256)
    return (y_prompt, y_sample, a_k_p, a_v_p, b_k_p, b_v_p, m_k_p, m_v_p, a_k_s, a_v_s, b_k_s, b_v_s)


def kernel(**inputs):
    inputs = {k: np.asarray(v) for k, v in inputs.items()}
    nc = build_program()
    sh = prep_shared(inputs)
    in_maps = [prep_core(inputs, c, sh) for c in range(8)]
    res = run_bass_kernel_spmd(nc, in_maps, core_ids=list(range(8)))
    return assemble(res.results)
```

```python
import contextlib
import numpy as np
import concourse.bass as bass
import concourse.mybir as mybir
from concourse.bass_utils import run_bass_kernel_spmd

F32 = mybir.dt.float32
BF16 = mybir.dt.bfloat16
AF = mybir.ActivationFunctionType
ALU = mybir.AluOpType

NL = 4
DM = 2048
DFF = 5632
SEQ = 16384
NTILE = 12
ZCOL = 12
ALPHA = (2 * NL) ** 0.25
EPS2 = 1e-5 / (ALPHA * ALPHA)
SCALE = 128.0 ** -0.5
MSCALE = 256.0 ** -0.5
NEG = -30000.0
NSLOT = 4

ENGS = ("pe", "act", "dve", "pool", "sp")
EPOCH = 30000
NEPOCH = 5
DMASETS = {("sp", "main"): 12, ("pool", "main"): 10, ("pool", "cast"): 4}


class Res:
    __slots__ = ("w", "r", "excl")

    def __init__(self, excl=False):
        self.w = None
        self.r = []
        self.excl = excl


def mkres(n):
    return [Res() for _ in range(n)]


class Ring:
    def __init__(self, n, base=0):
        self.n, self.i, self.base = n, 0, base

    def next(self):
        k = self.i
        self.i = (k + 1) % self.n
        return self.base + k


class Prog:
    def __init__(self, nc):
        self.nc = nc
        self.ops = {e: [] for e in ENGS}
        self.nsig = {e: 0 for e in ENGS}
        self.waited = {e: {} for e in ENGS}
        self.dma_rr = {k: 0 for k in DMASETS}
        self.dma_val = {}
        self.esem = {}
        self.dsem = {}

    def alloc_sems(self, stack):
        for e in ("pe", "act", "dve", "pool"):
            self.esem[e] = [stack.enter_context(self.nc.semaphore(f"s_{e}{i}")) for i in range(NEPOCH)]
        for (q, s), n in DMASETS.items():
            for k in range(n):
                self.dsem[(q, s, k)] = stack.enter_context(self.nc.semaphore(f"d_{q}_{s}{k}"))

    @staticmethod
    def _deps(reads, writes):
        deps = []
        for r in reads:
            if r.w is not None:
                deps.append(r.w)
            if r.excl:
                deps.extend(r.r)
        for w in writes:
            if w.w is not None:
                deps.append(w.w)
            deps.extend(w.r)
        return deps

    def _waits(self, eng, deps):
        need = {}
        for d in deps:
            if d[0] == "e":
                if d[1] == eng and eng == "pe":
                    continue
                key = ("e", d[1])
            else:
                key = ("d", d[1])
            if need.get(key, 0) < d[2]:
                need[key] = d[2]
        out = []
        wd = self.waited[eng]
        for key, val in need.items():
            if wd.get(key, 0) >= val:
                continue
            wd[key] = val
            out.append((key, val))
        return out

    @staticmethod
    def _commit(ev, reads, writes):
        for r in reads:
            r.r.append(ev)
        for w in writes:
            w.w = ev
            w.r = []

    def op(self, eng, fn, reads=(), writes=()):
        waits = self._waits(eng, self._deps(reads, writes))
        self.nsig[eng] += 1
        ev = ("e", eng, self.nsig[eng])
        self.ops[eng].append((waits, fn, ev))
        self._commit(ev, reads, writes)
        return ev

    def dma(self, q, out_ap, in_ap, reads=(), writes=(), semset="main"):
        deps = self._deps(reads, writes)
        k = self.dma_rr[(q, semset)]
        self.dma_rr[(q, semset)] = (k + 1) % DMASETS[(q, semset)]
        key = (q, semset, k)
        prev = self.dma_val.get(key, 0)
        if prev > 0:
            deps.append(("d", key, prev))
        waits = self._waits(q, deps)
        val = prev + 16
        self.dma_val[key] = val
        ev = ("d", key, val)
        self.ops[q].append((waits, (lambda e, o=out_ap, i=in_ap: e.dma_start(out=o, in_=i)), ev))
        self._commit(ev, reads, writes)
        return ev

    def finish(self):
        for q in ("sp", "pool"):
            deps = [("d", key, val) for key, val in self.dma_val.items() if key[0] == q]
            self.ops[q].append((self._waits(q, deps), None, None))

    def replay(self, eng, h):
        esem, dsem = self.esem, self.dsem
        for waits, fn, ev in self.ops[eng]:
            for key, val in waits:
                if key[0] == "e":
                    h.wait_ge(esem[key[1]][(val - 1) // EPOCH], (val - 1) % EPOCH + 1)
                else:
                    h.wait_ge(dsem[key[1]], val)
            if fn is None:
                continue
            ins = fn(h)
            if ev[0] == "e":
                ins.then_inc(esem[eng][(ev[2] - 1) // EPOCH], 1)
            else:
                ins.then_inc(dsem[ev[1]], 16)

    def run_block(self):
        self.finish()
        with self.nc.Block() as block:
            @block.tensor
            def _(h):
                self.replay("pe", h)

            @block.scalar
            def _(h):
                self.replay("act", h)

            @block.vector
            def _(h):
                self.replay("dve", h)

            @block.gpsimd
            def _(h):
                self.replay("pool", h)

            @block.sync
            def _(h):
                self.replay("sp", h)


def gen_slabs(get=None):
    def sl(src, k0, kc, c0, n):
        return None if get is None else get(src)[k0 * 128:(k0 + kc) * 128, c0:c0 + n]

    def ffn(w):
        for j in range(22):
            yield f"g{w}G{j}", 16, 256, sl(f"ffn{w}_gu", 0, 16, 256 * j, 256)
            yield f"g{w}U{j}", 16, 256, sl(f"ffn{w}_gu", 0, 16, DFF + 256 * j, 256)
        for mo in range(16):
            for hf in range(2):
                yield f"d{w}_{mo}_{hf}", 22, 128, sl(f"ffn{w}_down", 22 * hf, 22, 128 * mo, 128)

    yield from ffn(1)
    for j in range(4):
        yield f"ka{j}", 16, 256, sl("w_in", 0, 16, 1024 + 256 * j, 256)
    for j in range(4):
        yield f"va{j}", 16, 256, sl("w_in", 0, 16, 2048 + 256 * j, 256)
    yield "kb", 16, 256, sl("w_in", 0, 16, 4096, 256)
    yield "vb", 16, 256, sl("w_in", 0, 16, 4352, 256)
    for j in range(4):
        yield f"qa{j}", 16, 256, sl("w_in", 0, 16, 256 * j, 256)
    for j in range(4):
        yield f"qb{j}", 16, 256, sl("w_in", 0, 16, 3072 + 256 * j, 256)
    for j in range(4):
        yield f"qm{j}", 16, 256, sl("w_in", 0, 16, 4608 + 256 * j, 256)
    for m in range(16):
        ga = sl("w_in", 0, 16, 5632 + 128 * m, 128)
        gb = sl("w_in", 0, 16, 5632 + 2048 + 128 * m, 128)
        yield f"gab{m}", 16, 256, (None if get is None else np.concatenate([ga, gb], axis=1))
        yield f"gm{m}", 16, 128, sl("w_in", 0, 16, 5632 + 4096 + 128 * m, 128)
        ba = sl("w_br_a", 0, 8, 128 * m, 128)
        bb = sl("w_br_b", 0, 8, 128 * m, 128)
        bm = sl("w_br_m", 0, 8, 128 * m, 128)
        yield f"br{m}", 24, 128, (None if get is None else np.concatenate([ba, bb, bm], axis=0))
    for j in range(8):
        yield f"wo{j}", 16, 256, sl("w_out", 0, 16, 256 * j, 256)
    yield from ffn(2)
    for j in range(4):
        yield f"mk{j}", 16, 256, sl("w_mem_kv", 0, 16, 256 * j, 256)
    for j in range(4):
        yield f"mv{j}", 16, 256, sl("w_mem_kv", 0, 16, 1024 + 256 * j, 256)


def slab_layout():
    SL = {}
    off = 0
    pieces = []
    pstart = 0
    for name, KC, C, _ in gen_slabs():
        n = KC * C
        if off + n - pstart > 49152:
            pieces.append((pstart, off))
            pstart = off
        SL[name] = (off, KC, C, len(pieces))
        off += n
    pieces.append((pstart, off))
    return SL, off, pieces


def host_weight_stream(inputs, l):
    get = lambda name: inputs[name][l]
    parts = []
    for name, KC, C, d in gen_slabs(get):
        parts.append(np.ascontiguousarray(d.reshape(KC, 128, C).transpose(1, 0, 2)).reshape(128, KC * C))
    return np.ascontiguousarray(np.concatenate(parts, axis=1), dtype=np.float32)


def build_program(nlayers=NL, ti0=0):
    nc = bass.Bass("TRN2", target_bir_lowering=False)
    SL, E, PIECES = slab_layout()
    NP = len(PIECES)

    def din(name, shape):
        return nc.dram_tensor(name, shape, F32, kind="ExternalInput")

    def dout(name, shape):
        return nc.dram_tensor(name, shape, F32, kind="ExternalOutput")

    xin_d = din("xin", [NTILE, 128, 16, 512]).ap()
    xs_d = din("xs", [128, 16, 16]).ap()
    kmask_d = din("kmask", [128, 16]).ap()
    WS = [din(f"ws{l}", [128, E]).ap() for l in range(nlayers)]
    lnp_d = din("lnp", [128, NL * 3 * 2 * 16]).ap()
    extA_t = din("extA", [NL, 8, 768])
    maskA_d = din("maskA", [128, 640]).ap()
    biasB_d = din("biasB", [8, 128, 256]).ap()
    sink_t = din("sink", [NL * 8])
    memT_d = din("memT", [128, 16, 256]).ap()
    cakT_d = din("cakT", [NL, 128, 8, 512]).ap()
    cav_d = din("cav", [NL, 512, 1024]).ap()
    cbkT_d = din("cbkT", [NL, 128, 2, 128]).ap()
    cbv_d = din("cbv", [NL, 128, 256]).ap()
    cmkT_d = din("cmkT", [NL, 128, 8, 256]).ap()
    cmv_d = din("cmv", [NL, 256, 1024]).ap()

    y_out = dout("y_out", [8, 128, 16, 512]).ap()
    ys_out = dout("ys_out", [128, 16, 16]).ap()
    akp = dout("akp", [NL, 128, 8, 512]).ap()
    avp = dout("avp", [NL, 512, 1024]).ap()
    bkp = dout("bkp", [NL, 128, 2, 128]).ap()
    bvp = dout("bvp", [NL, 128, 256]).ap()
    mkp = dout("mkp", [NL, 128, 8, 256]).ap()
    mvp = dout("mvp", [NL, 256, 1024]).ap()
    aks = dout("aks", [NL, 128, 8, 16]).ap()
    avs = dout("avs", [NL, 16, 1024]).ap()
    bks = dout("bks", [NL, 128, 2, 16]).ap()
    bvs = dout("bvs", [NL, 16, 256]).ap()

    WSb = [nc.dram_tensor(f"wsb{l}", [128, E], BF16, kind="Internal").ap() for l in range(nlayers)]
    xres_d = nc.dram_tensor("xres", [NTILE, 128, 16, 512], F32, kind="Internal").ap()
    xres_s = nc.dram_tensor("xress", [128, 16, 16], F32, kind="Internal").ap()
    Dtoe_t = [nc.dram_tensor(f"dtoe{l}", [8, 128, 768], F32, kind="Internal") for l in range(nlayers)]
    biasA_d = nc.dram_tensor("biasA", [NL, 8, 128, 640], F32, kind="Internal").ap()

    P = Prog(nc)
    with contextlib.ExitStack() as st:
        P.alloc_sems(st)

        def sb(name, shape, dtype):
            return st.enter_context(nc.sbuf_tensor(name, shape, dtype))

        x32 = sb("x32", [128, 16, 512], F32)
        xb = sb("xb", [128, 16, 512], BF16)
        U = sb("U", [128, 44 * 512], BF16)
        KA = sb("KA", [128, 2, 8, 512], BF16)
        VA = sb("VA", [128, 2, 4, 1024], BF16)
        KB = sb("KB", [128, 2, 2, 512], BF16)
        VB = sb("VB", [128, 2, 4, 256], BF16)
        wring = sb("wring", [128, NSLOT, 4096], BF16)
        bAr = sb("bAr", [128, 2, 640], F32)
        bBr = sb("bBr", [128, 2, 256], F32)
        memKT = sb("memKT", [128, 8, 256], BF16)
        memV = sb("memV", [128, 2, 1024], BF16)
        actT = sb("actT", [128, 3, 512], F32)
        dveT = sb("dveT", [128, 4, 512], F32)
        PT = sb("PT", [128, 4, 512], BF16)
        lnT = sb("lnT", [128, 3, 512], F32)
        ones32 = sb("ones32", [128, 128], F32)
        onesb = sb("onesb", [128, 128], BF16)
        lnp = sb("lnp_s", [128, NL * 3 * 2 * 16], F32)
        kmask = sb("kmask_s", [128, 16], F32)
        esink = sb("esink", [128, NL * 8], F32)
        epsc = sb("epsc", [128, 1], F32)
        ps = st.enter_context(nc.psum_tensor("ps", [128, 8, 512], F32))

        r_x32, r_xb, r_u = mkres(16), mkres(16), mkres(44)
        r_KA = [mkres(8), mkres(8)]
        r_VA = [mkres(4), mkres(4)]
        r_KB = [mkres(2), mkres(2)]
        r_VB = [mkres(4), mkres(4)]
        r_w, r_bA, r_bB = mkres(NSLOT), mkres(2), mkres(2)
        r_memKT, r_memV = mkres(8), mkres(2)
        r_actT, r_dveT, r_PT, r_ln = mkres(3), mkres(4), mkres(4), mkres(3)
        r_ps = [Res(excl=True) for _ in range(8)]
        r_const = Res()
        r_wsb = [mkres(NP) for _ in range(nlayers)]
        r_xres = mkres(NTILE)
        r_xres_s = Res()
        r_dtoe = mkres(nlayers)
        r_biasA = [mkres(8) for _ in range(nlayers)]

        bankR, wR, actR, dveR, ptR = Ring(8), Ring(NSLOT), Ring(3), Ring(4), Ring(4)
        bAR, bBR, sR = Ring(2), Ring(2), Ring(4)
        evtog = [0]

        def mm(bank, pairs, reads, M=128, N=512):
            out = ps[0:M, bank, 0:N]
            n = len(pairs)

            def fn(e, out=out, pairs=pairs, n=n):
                ins = None
                for i, (lt, rh) in enumerate(pairs):
                    ins = e.matmul(out, lt, rh, start=(i == 0), stop=(i == n - 1))
                return ins
            P.op("pe", fn, reads=reads, writes=[r_ps[bank]])

        def mm1(out, lt, rh, start, stop, reads, bank):
            P.op("pe", lambda e, o=out, a=lt, b=rh, s=start, t=stop: e.matmul(o, a, b, start=s, stop=t),
                 reads=reads, writes=[r_ps[bank]])

        def evac(out, in_, reads, writes, eng=None):
            if eng is None:
                eng = "act" if evtog[0] == 0 else "dve"
                evtog[0] ^= 1
            if eng == "act":
                P.op("act", lambda e, o=out, i=in_: e.activation(out=o, in_=i, func=AF.Copy), reads, writes)
            else:
                P.op("dve", lambda e, o=out, i=in_: e.tensor_copy(out=o, in_=i), reads, writes)

        def out_f32(dram_ap, psum_ap, p, n, rbank):
            k = dveR.next()
            evac(dveT[0:p, k, 0:n], psum_ap, [rbank], [r_dveT[k]])
            P.dma("pool", dram_ap, dveT[0:p, k, 0:n], reads=[r_dveT[k]])

        def ld(l, name):
            off, KC, C, piece = SL[name]
            k = wR.next()
            P.dma("sp", wring[:, k, 0:KC * C], WSb[l][:, off:off + KC * C], reads=[r_wsb[l][piece]], writes=[r_w[k]])
            return wring[:, k, 0:KC * C].rearrange("p (kc c) -> p kc c", c=C), r_w[k]

        def uch(idx, T, lo=0):
            return U[:, idx * 512 + lo: idx * 512 + T]

        def ffn(l, w, T, mid_hook=None):
            for j in range(22):
                G, rG = ld(l, f"g{w}G{j}")
                Uw, rU = ld(l, f"g{w}U{j}")
                bg = [bankR.next(), bankR.next()]
                bu = [bankR.next(), bankR.next()]
                if j == 0:
                    for kc in range(16):
                        def fn(e, kc=kc, G=G, Uw=Uw, bg=bg, bu=bu):
                            ins = None
                            for c in range(2):
                                ins = e.matmul(ps[:, bg[c], 0:T], G[:, kc, c * 128:(c + 1) * 128], xb[:, kc, 0:T], start=(kc == 0), stop=(kc == 15))
                            for c in range(2):
                                ins = e.matmul(ps[:, bu[c], 0:T], Uw[:, kc, c * 128:(c + 1) * 128], xb[:, kc, 0:T], start=(kc == 0), stop=(kc == 15))
                            return ins
                        P.op("pe", fn, [rG, rU, r_xb[kc]], [r_ps[bg[0]], r_ps[bg[1]], r_ps[bu[0]], r_ps[bu[1]]])
                else:
                    for c in range(2):
                        mm(bg[c], [(G[:, kc, c * 128:(c + 1) * 128], xb[:, kc, 0:T]) for kc in range(16)], [rG] + r_xb, N=T)
                    for c in range(2):
                        mm(bu[c], [(Uw[:, kc, c * 128:(c + 1) * 128], xb[:, kc, 0:T]) for kc in range(16)], [rU] + r_xb, N=T)
                for c in range(2):
                    m = 2 * j + c
                    t = actR.next()
                    P.op("act", lambda e, o=actT[:, t, 0:T], i=ps[:, bg[c], 0:T]: e.activation(out=o, in_=i, func=AF.Silu),
                         [r_ps[bg[c]]], [r_actT[t]])
                    P.op("dve", lambda e, o=uch(m, T), a=ps[:, bu[c], 0:T], b=actT[:, t, 0:T]:
                         e.tensor_tensor(out=o, in0=a, in1=b, op=ALU.mult),
                         [r_ps[bu[c]], r_actT[t]], [r_u[m]])
            if mid_hook is not None:
                mid_hook()
            for mo in range(16):
                D0, r0 = ld(l, f"d{w}_{mo}_0")
                D1, r1 = ld(l, f"d{w}_{mo}_1")
                b = bankR.next()
                pairs = [(D0[:, kc, :], uch(kc, T)) for kc in range(22)] + [(D1[:, kc, :], uch(22 + kc, T)) for kc in range(22)]
                mm(b, pairs, [r0, r1] + r_u, N=T)
                P.op("dve", lambda e, o=x32[:, mo, 0:T], a=ps[:, b, 0:T]:
                     e.scalar_tensor_tensor(out=o, in0=a, scalar=0.5 / ALPHA, in1=o, op0=ALU.mult, op1=ALU.add),
                     [r_ps[b], r_x32[mo]], [r_x32[mo]])
                accum_stats(mo, T)

        def accum_stats(mo, T):
            s1, s2, xc = lnT[:, 0, 0:T], lnT[:, 2, 0:T], x32[:, mo, 0:T]
            if mo == 0:
                P.op("dve", lambda e: e.tensor_copy(out=s1, in_=xc), [r_x32[mo]], [r_ln[0]])
                P.op("act", lambda e: e.activation(out=s2, in_=xc, func=AF.Square), [r_x32[mo]], [r_ln[2]])
            else:
                t = actR.next()
                sq = actT[:, t, 0:T]
                P.op("act", lambda e: e.activation(out=sq, in_=xc, func=AF.Square), [r_x32[mo]], [r_actT[t]])
                P.op("dve", lambda e: e.tensor_tensor(out=s1, in0=s1, in1=xc, op=ALU.add), [r_x32[mo], r_ln[0]], [r_ln[0]])
                P.op("dve", lambda e: e.tensor_tensor(out=s2, in0=s2, in1=sq, op=ALU.add), [r_actT[t], r_ln[2]], [r_ln[2]])

        def layernorm(l, i, T, write_xb=True):
            b1, b2 = bankR.next(), bankR.next()
            mm1(ps[:, b1, 0:T], ones32[:], lnT[:, 0, 0:T], True, True, [r_ln[0], r_const], b1)
            mm1(ps[:, b2, 0:T], ones32[:], lnT[:, 2, 0:T], True, True, [r_ln[2], r_const], b2)
            mean, rstd, nmr = lnT[:, 0, 0:T], lnT[:, 1, 0:T], lnT[:, 2, 0:T]
            P.op("dve", lambda e: e.tensor_scalar(out=mean, in0=ps[:, b1, 0:T], scalar1=1.0 / DM, scalar2=None, op0=ALU.mult),
                 [r_ps[b1]], [r_ln[0]])
            P.op("dve", lambda e: e.tensor_tensor(out=nmr, in0=mean, in1=mean, op=ALU.mult), [r_ln[0]], [r_ln[2]])
            P.op("dve", lambda e: e.scalar_tensor_tensor(out=nmr, in0=ps[:, b2, 0:T], scalar=1.0 / DM, in1=nmr,
                                                         op0=ALU.mult, op1=ALU.subtract), [r_ps[b2], r_ln[2]], [r_ln[2]])
            P.op("act", lambda e: e.activation(out=nmr, in_=nmr, func=AF.Sqrt, bias=epsc[:, 0:1], scale=1.0),
                 [r_ln[2], r_const], [r_ln[2]])
            P.op("dve", lambda e: e.reciprocal(out=rstd, in_=nmr), [r_ln[2]], [r_ln[1]])
            P.op("dve", lambda e: e.scalar_tensor_tensor(out=nmr, in0=mean, scalar=-1.0, in1=rstd, op0=ALU.mult, op1=ALU.mult),
                 [r_ln[0], r_ln[1]], [r_ln[2]])
            base = (l * 3 + i) * 32
            for mo in range(16):
                xc = x32[:, mo, 0:T]
                g_ap = lnp[:, base + mo: base + mo + 1]
                b_ap = lnp[:, base + 16 + mo: base + 16 + mo + 1]
                P.op("dve", lambda e, xc=xc: e.tensor_tensor(out=xc, in0=xc, in1=rstd, op=ALU.mult), [r_x32[mo], r_ln[1]], [r_x32[mo]])
                P.op("dve", lambda e, xc=xc: e.tensor_tensor(out=xc, in0=xc, in1=nmr, op=ALU.add), [r_x32[mo], r_ln[2]], [r_x32[mo]])
                if write_xb:
                    P.op("act", lambda e, xc=xc, o=xb[:, mo, 0:T], g=g_ap, b=b_ap: e.activation(out=o, in_=xc, func=AF.Identity, bias=b, scale=g),
                         [r_x32[mo], r_const], [r_xb[mo]])
                P.op("act", lambda e, xc=xc, g=g_ap, b=b_ap: e.activation(out=xc, in_=xc, func=AF.Identity, bias=b, scale=g),
                     [r_x32[mo], r_const], [r_x32[mo]])

        def proj_kv(l, T, half, outs):
            nb = max(1, T // 128)
            tw = min(T, 128)
            for j in range(4):
                W, r = ld(l, f"ka{j}")
                for c in range(2):
                    h = 2 * j + c
                    b = bankR.next()
                    mm(b, [(W[:, kc, c * 128:(c + 1) * 128], xb[:, kc, 0:T]) for kc in range(16)], [r] + r_xb, N=T)
                    evac(KA[:, half, h, 0:T], ps[:, b, 0:T], [r_ps[b]], [r_KA[half][h]])
                    if outs:
                        out_f32(outs["ak"][:, h, :], ps[:, b, 0:T], 128, T, r_ps[b])
            for j in range(4):
                W, r = ld(l, f"va{j}")
                for tb in range(nb):
                    b = bankR.next()
                    mm(b, [(xb[:, kc, tb * 128: tb * 128 + tw], W[:, kc, :]) for kc in range(16)], [r] + r_xb, M=tw, N=256)
                    evac(VA[0:tw, half, tb, j * 256:(j + 1) * 256], ps[0:tw, b, 0:256], [r_ps[b]], [r_VA[half][tb]])
                    if outs:
                        out_f32(outs["av"][tb * 128: tb * 128 + tw, j * 256:(j + 1) * 256], ps[0:tw, b, 0:256], tw, 256, r_ps[b])
            W, r = ld(l, "kb")
            for g in range(2):
                b = bankR.next()
                mm(b, [(W[:, kc, g * 128:(g + 1) * 128], xb[:, kc, 0:T]) for kc in range(16)], [r] + r_xb, N=T)
                evac(KB[:, half, g, 0:T], ps[:, b, 0:T], [r_ps[b]], [r_KB[half][g]])
                if outs:
                    lo = max(0, T - 128)
                    out_f32(outs["bk"][:, g, :], ps[:, b, lo:T], 128, T - lo, r_ps[b])
            W, r = ld(l, "vb")
            for tb in range(nb):
                b = bankR.next()
                mm(b, [(xb[:, kc, tb * 128: tb * 128 + tw], W[:, kc, :]) for kc in range(16)], [r] + r_xb, M=tw, N=256)
                evac(VB[0:tw, half, tb, :], ps[0:tw, b, 0:256], [r_ps[b]], [r_VB[half][tb]])
                if outs and tb == nb - 1:
                    out_f32(outs["bv"], ps[0:tw, b, 0:256], tw, 256, r_ps[b])

        def proj_q(l, prefix, T):
            for j in range(4):
                W, r = ld(l, f"{prefix}{j}")
                for c in range(2):
                    idx = 2 * j + c
                    b = bankR.next()
                    mm(b, [(W[:, kc, c * 128:(c + 1) * 128], xb[:, kc, 0:T]) for kc in range(16)], [r] + r_xb, N=T)
                    evac(uch(24 + idx, T), ps[:, b, 0:T], [r_ps[b]], [r_u[24 + idx]])

        def attend(l, T, kind, blocks):
            nblk = len(blocks)
            items = [(h, bi) for h in range(8) for bi in range(nblk)]
            LA = 3
            state = {}

            def stage1(h, bi):
                if bi == 0:
                    if kind == "A":
                        s = bAR.next()
                        P.dma("sp", bAr[:, s, :], biasA_d[l, h], reads=[r_biasA[l][h]], writes=[r_bA[s]])
                        state[h] = (bAr[:, s, :], r_bA[s])
                    else:
                        s = bBR.next()
                        P.dma("sp", bBr[:, s, :], biasB_d[h], writes=[r_bB[s]])
                        state[h] = (bBr[:, s, :], r_bB[s])
                bias, rb = state[h]
                half, blk, nk, qlo, qhi, bc0, kmcol, _ = blocks[bi]
                nq = qhi - qlo
                bs = sR.next()
                if kind == "A":
                    kt, rk = KA[:, half, h, blk * 128: blk * 128 + nk], r_KA[half][h]
                else:
                    kt, rk = KB[:, half, h // 4, blk * 128: blk * 128 + nk], r_KB[half][h // 4]
                mm1(ps[0:nk, bs, 0:nq], kt, uch(24 + h, qhi, qlo), True, True, [rk, r_u[24 + h]], bs)
                d = dveR.next()
                P.op("dve", lambda e, o=dveT[0:nk, d, 0:nq], a=ps[0:nk, bs, 0:nq], b=bias[0:nk, bc0:bc0 + nq]:
                     e.scalar_tensor_tensor(out=o, in0=a, scalar=SCALE, in1=b, op0=ALU.mult, op1=ALU.add),
                     [r_ps[bs], rb], [r_dveT[d]])
                p = ptR.next()
                P.op("act", lambda e, o=PT[0:nk, p, 0:nq], a=dveT[0:nk, d, 0:nq], km=kmask[0:nk, kmcol:kmcol + 1]:
                     e.activation(out=o, in_=a, func=AF.Exp, bias=km, scale=1.0),
                     [r_dveT[d], r_const], [r_PT[p]])
                return p

            def stage2(h, bi, p):
                half, blk, nk, qlo, qhi, bc0, kmcol, stf = blocks[bi]
                nq = qhi - qlo
                bO, bD = 4 + (h % 2), 6 + (h % 2)
                last = bi == nblk - 1
                if kind == "A":
                    v, rv = VA[0:nk, half, blk, h * 128:(h + 1) * 128], r_VA[half][blk]
                else:
                    g = h // 4
                    v, rv = VB[0:nk, half, blk, g * 128:(g + 1) * 128], r_VB[half][blk]
                mm1(ps[:, bO, qlo:qhi], v, PT[0:nk, p, 0:nq], stf, last, [rv, r_PT[p]], bO)
                mm1(ps[:, bD, qlo:qhi], onesb[0:nk, :], PT[0:nk, p, 0:nq], stf, last, [r_PT[p], r_const], bD)
                if last:
                    d = dveR.next()
                    addv = 1e-30 if kind == "A" else esink[:, l * 8 + h: l * 8 + h + 1]
                    P.op("dve", lambda e, o=dveT[:, d, 0:T], a=ps[:, bD, 0:T], s=addv:
                         e.tensor_scalar(out=o, in0=a, scalar1=s, scalar2=None, op0=ALU.add),
                         [r_ps[bD], r_const], [r_dveT[d]])
                    P.op("dve", lambda e, o=dveT[:, d, 0:T]: e.reciprocal(out=o, in_=o), [r_dveT[d]], [r_dveT[d]])
                    oidx = h if kind == "A" else 8 + h
                    P.op("dve", lambda e, o=uch(oidx, T), a=ps[:, bO, 0:T], b=dveT[:, d, 0:T]:
                         e.tensor_tensor(out=o, in0=a, in1=b, op=ALU.mult),
                         [r_ps[bO], r_dveT[d]], [r_u[oidx]])

            pend = []
            for k in range(len(items) + LA):
                if k < len(items):
                    pend.append(stage1(*items[k]))
                if k >= LA:
                    stage2(items[k - LA][0], items[k - LA][1], pend[k - LA])

        def attend_m(l, T):
            for hm in range(4):
                pts = []
                for mb in range(2):
                    bs = sR.next()
                    prs = [(memKT[:, hm * 2 + hh, mb * 128:(mb + 1) * 128], uch(24 + hm * 2 + hh, T)) for hh in range(2)]
                    mm(bs, prs, [r_memKT[hm * 2], r_memKT[hm * 2 + 1], r_u[24 + hm * 2], r_u[24 + hm * 2 + 1]], N=T)
                    p = ptR.next()
                    P.op("act", lambda e, o=PT[:, p, 0:T], a=ps[:, bs, 0:T]: e.activation(out=o, in_=a, func=AF.Exp, scale=MSCALE),
                         [r_ps[bs]], [r_PT[p]])
                    pts.append(p)
                bD = 6 + hm % 2
                rpt = [r_PT[pts[0]], r_PT[pts[1]]]
                for dhh in range(2):
                    c0 = hm * 256 + dhh * 128
                    mm(4 + dhh, [(memV[:, mb, c0:c0 + 128], PT[:, pts[mb], 0:T]) for mb in range(2)], r_memV + rpt, N=T)
                mm(bD, [(onesb[:], PT[:, pts[mb], 0:T]) for mb in range(2)], rpt + [r_const], N=T)
                d = dveR.next()
                P.op("dve", lambda e, o=dveT[:, d, 0:T], a=ps[:, bD, 0:T]: e.reciprocal(out=o, in_=a), [r_ps[bD]], [r_dveT[d]])
                for dhh in range(2):
                    oidx = 16 + hm * 2 + dhh
                    P.op("dve", lambda e, o=uch(oidx, T), a=ps[:, 4 + dhh, 0:T], b=dveT[:, d, 0:T]:
                         e.tensor_tensor(out=o, in0=a, in1=b, op=ALU.mult),
                         [r_ps[4 + dhh], r_dveT[d]], [r_u[oidx]])

        def merge(l, T):
            for m in range(16):
                Wab, rab = ld(l, f"gab{m}")
                Wm, rm = ld(l, f"gm{m}")
                Wbr, rbr = ld(l, f"br{m}")
                bg = [bankR.next() for _ in range(3)]
                mm(bg[0], [(Wab[:, kc, 0:128], xb[:, kc, 0:T]) for kc in range(16)], [rab] + r_xb, N=T)
                mm(bg[1], [(Wab[:, kc, 128:256], xb[:, kc, 0:T]) for kc in range(16)], [rab] + r_xb, N=T)
                mm(bg[2], [(Wm[:, kc, :], xb[:, kc, 0:T]) for kc in range(16)], [rm] + r_xb, N=T)
                bb = [bankR.next() for _ in range(3)]
                for br in range(3):
                    mm(bb[br], [(Wbr[:, br * 8 + kc, :], uch(br * 8 + kc, T)) for kc in range(8)],
                       [rbr] + r_u[br * 8: br * 8 + 8], N=T)
                sg = []
                for br in range(3):
                    t = actR.next()
                    P.op("act", lambda e, o=actT[:, t, 0:T], a=ps[:, bg[br], 0:T]: e.activation(out=o, in_=a, func=AF.Sigmoid),
                         [r_ps[bg[br]]], [r_actT[t]])
                    sg.append(t)
                d1, d2 = dveR.next(), dveR.next()
                t1, t2 = dveT[:, d1, 0:T], dveT[:, d2, 0:T]
                P.op("dve", lambda e, t1=t1, a=ps[:, bb[0], 0:T], b=actT[:, sg[0], 0:T]: e.tensor_tensor(out=t1, in0=a, in1=b, op=ALU.mult),
                     [r_ps[bb[0]], r_actT[sg[0]]], [r_dveT[d1]])
                P.op("dve", lambda e, t2=t2, a=ps[:, bb[1], 0:T], b=actT[:, sg[1], 0:T]: e.tensor_tensor(out=t2, in0=a, in1=b, op=ALU.mult),
                     [r_ps[bb[1]], r_actT[sg[1]]], [r_dveT[d2]])
                P.op("dve", lambda e, t1=t1, t2=t2: e.tensor_tensor(out=t1, in0=t1, in1=t2, op=ALU.add), [r_dveT[d1], r_dveT[d2]], [r_dveT[d1]])
                P.op("dve", lambda e, t2=t2, a=ps[:, bb[2], 0:T], b=actT[:, sg[2], 0:T]: e.tensor_tensor(out=t2, in0=a, in1=b, op=ALU.mult),
                     [r_ps[bb[2]], r_actT[sg[2]], r_dveT[d2]], [r_dveT[d2]])
                P.op("dve", lambda e, t1=t1, t2=t2, o=uch(24 + m, T): e.tensor_tensor(out=o, in0=t1, in1=t2, op=ALU.add),
                     [r_dveT[d1], r_dveT[d2]], [r_u[24 + m]])

        def wout(l, T):
            for j in range(8):
                W, r = ld(l, f"wo{j}")
                for c in range(2):
                    mo = 2 * j + c
                    b = bankR.next()
                    mm(b, [(W[:, kc, c * 128:(c + 1) * 128], uch(24 + kc, T)) for kc in range(16)], [r] + r_u[24:40], N=T)
                    P.op("dve", lambda e, o=x32[:, mo, 0:T], a=ps[:, b, 0:T]:
                         e.scalar_tensor_tensor(out=o, in0=a, scalar=1.0 / ALPHA, in1=o, op0=ALU.mult, op1=ALU.add),
                         [r_ps[b], r_x32[mo]], [r_x32[mo]])
                    accum_stats(mo, T)

        def tile_layer(l, T, half, full, blocksA, blocksB, outs, mid_hook=None):
            ffn(l, 1, T)
            layernorm(l, 0, T)
            proj_kv(l, T, half, outs)
            if not full:
                return
            proj_q(l, "qa", T)
            attend(l, T, "A", blocksA)
            proj_q(l, "qb", T)
            attend(l, T, "B", blocksB)
            proj_q(l, "qm", T)
            attend_m(l, T)
            merge(l, T)
            wout(l, T)
            layernorm(l, 1, T)
            ffn(l, 2, T, mid_hook)
            layernorm(l, 2, T, write_xb=False)

        def mem_kv(l):
            memT = U[:, 0:4096].rearrange("p (kc m) -> p kc m", m=256)
            P.dma("pool", memT, memT_d, writes=r_u[0:8])
            for j in range(4):
                W, r = ld(l, f"mk{j}")
                for c in range(2):
                    idx = 2 * j + c
                    b = bankR.next()
                    mm(b, [(W[:, kc, c * 128:(c + 1) * 128], memT[:, kc, :]) for kc in range(16)], [r] + r_u[0:8], N=256)
                    evac(memKT[:, idx, :], ps[:, b, 0:256], [r_ps[b]], [r_memKT[idx]])
                    out_f32(mkp[l, :, idx, :], ps[:, b, 0:256], 128, 256, r_ps[b])
            for j in range(4):
                W, r = ld(l, f"mv{j}")
                for mb in range(2):
                    b = bankR.next()
                    mm(b, [(memT[:, kc, mb * 128:(mb + 1) * 128], W[:, kc, :]) for kc in range(16)], [r] + r_u[0:8], N=256)
                    evac(memV[:, mb, j * 256:(j + 1) * 256], ps[:, b, 0:256], [r_ps[b]], [r_memV[mb]])
                    out_f32(mvp[l, mb * 128:(mb + 1) * 128, j * 256:(j + 1) * 256], ps[:, b, 0:256], 128, 256, r_ps[b])

        P.op("dve", lambda e: e.memset(ones32[:], 1.0), writes=[r_const])
        P.op("dve", lambda e: e.memset(onesb[:], 1.0), writes=[r_const])
        P.op("dve", lambda e: e.memset(epsc[:], EPS2), writes=[r_const])
        P.dma("sp", lnp[:], lnp_d, writes=[r_const])
        P.dma("sp", kmask[:], kmask_d, writes=[r_const])
        maskA = lnT[:, 0:2, :].rearrange("p a b -> p (a b)")[:, 0:640]
        P.dma("sp", maskA, maskA_d, writes=[r_ln[0], r_ln[1]])
        P.dma("sp", esink[:], bass.AP(sink_t, 0, [[0, 128], [1, NL * 8]]), writes=[r_const])
        P.op("act", lambda e: e.activation(out=esink[:], in_=esink[:], func=AF.Exp), [r_const], [r_const])

        def emit_cast(l, pi):
            a, b = PIECES[pi]
            P.dma("pool", WSb[l][:, a:b], WS[l][:, a:b], writes=[r_wsb[l][pi]], semset="cast")

        for pi in range(NP):
            emit_cast(0, pi)

        for l in range(nlayers):
            P.dma("sp", Dtoe_t[l].ap(), bass.AP(extA_t, l * 8 * 768, [[768, 8], [0, 128], [1, 768]]), writes=[r_dtoe[l]])
            for h in range(8):
                s = bAR.next()
                P.dma("sp", bAr[:, s, :], bass.AP(Dtoe_t[l], h * 128 * 768 + 127, [[767, 128], [1, 640]]),
                      reads=[r_dtoe[l]], writes=[r_bA[s]])
                P.op("dve", lambda e, o=bAr[:, s, :]: e.tensor_tensor(out=o, in0=o, in1=maskA, op=ALU.add),
                     [r_bA[s], r_ln[0], r_ln[1]], [r_bA[s]])
                P.dma("pool", biasA_d[l, h], bAr[:, s, :], reads=[r_bA[s]], writes=[r_biasA[l][h]])

        Z = ZCOL
        sblkA = [(0, 0, 128, 0, 16, 512, Z, True), (0, 1, 128, 0, 16, 384, Z, False), (0, 2, 128, 0, 16, 256, Z, False),
                 (0, 3, 128, 0, 16, 128, Z, False), (1, 0, 16, 0, 16, 0, Z, False)]
        sblkB = [(0, 3, 128, 0, 16, 128, Z, True), (1, 0, 16, 0, 16, 0, Z, False)]

        xb_prefetched = [False]
        for l in range(nlayers):
            lastl = l == nlayers - 1
            src = xs_d if l == 0 else xres_s
            P.dma("sp", x32[:, :, 0:16], src, reads=[r_xres_s], writes=r_x32)
            P.dma("pool", xb[:, :, 0:16], src, reads=[r_xres_s], writes=r_xb)
            P.dma("pool", KA[:, 0, :, :], cakT_d[l], writes=r_KA[0])
            P.dma("pool", VA[:, 0, :, :], cav_d[l].rearrange("(blk p) c -> p blk c", p=128), writes=r_VA[0])
            P.dma("pool", KB[:, 0, :, 384:512], cbkT_d[l], writes=r_KB[0])
            P.dma("pool", VB[:, 0, 3, :], cbv_d[l], writes=[r_VB[0][3]])
            P.dma("pool", memKT[:], cmkT_d[l], writes=r_memKT)
            P.dma("pool", memV[:], cmv_d[l].rearrange("(mb p) c -> p mb c", p=128), writes=r_memV)
            souts = {"ak": aks[l], "av": avs[l], "bk": bks[l], "bv": bvs[l]}
            tile_layer(l, 16, 1, True, sblkA, sblkB, souts)
            P.dma("pool", ys_out if lastl else xres_s, x32[:, :, 0:16], reads=r_x32, writes=[r_xres_s])
            mem_kv(l)
            first = ti0 + l
            casts = [(l + 1, pi) for pi in range(NP)] if l + 1 < nlayers else []
            ntl = NTILE - first
            for ti in range(first, NTILE):
                src = xin_d[ti] if l == 0 else xres_d[ti]
                P.dma("pool", x32[:], src, reads=[r_xres[ti]], writes=r_x32)
                if not xb_prefetched[0]:
                    P.dma("pool", xb[:], src, reads=[r_xres[ti]], writes=r_xb)
                xb_prefetched[0] = False
                full = ti >= first + 1

                def hook(ti=ti, l=l):
                    if ti + 1 < NTILE:
                        nsrc = xin_d[ti + 1] if l == 0 else xres_d[ti + 1]
                        P.dma("pool", xb[:], nsrc, reads=[r_xres[ti + 1]], writes=r_xb)
                        xb_prefetched[0] = True
                cur, prv = ti % 2, 1 - ti % 2
                bA = [(prv, 3, 128, 0, 512, 128, ti - 1, True), (prv, 0, 128, 0, 128, 512, ti - 1, False),
                      (prv, 1, 128, 0, 256, 384, ti - 1, False), (prv, 2, 128, 0, 384, 256, ti - 1, False),
                      (cur, 1, 128, 128, 512, 0, ti, False), (cur, 2, 128, 256, 512, 0, ti, False),
                      (cur, 3, 128, 384, 512, 0, ti, False), (cur, 0, 128, 0, 512, 0, ti, False)]
                bB = [(cur, 0, 128, 0, 256, 0, ti, True), (cur, 2, 128, 256, 512, 0, ti, False),
                      (prv, 3, 128, 0, 128, 128, ti - 1, False), (cur, 1, 128, 128, 384, 0, ti, False),
                      (cur, 3, 128, 384, 512, 0, ti, False)]
                pouts = {"ak": akp[l], "av": avp[l], "bk": bkp[l], "bv": bvp[l]} if ti == NTILE - 1 else None
                tile_layer(l, 512, cur, full, bA, bB, pouts, hook if full else None)
                if full:
                    if lastl:
                        if ti >= 4:
                            P.dma("pool", y_out[ti - 4], x32[:], reads=r_x32, writes=[r_xres[ti]])
                    else:
                        P.dma("pool", xres_d[ti], x32[:], reads=r_x32, writes=[r_xres[ti]])
                left = NTILE - ti
                ncast = (len(casts) + left - 1) // left
                for _ in range(ncast):
                    emit_cast(*casts.pop(0))
        P.run_block()
    return nc


def _const_masks():
    ki = np.arange(128)[:, None]
    qa = np.arange(640)[None, :]
    da = qa // 64 - ki // 64
    maskA = np.where((da >= 0) & (da <= 8), 0.0, NEG).astype(np.float32)
    qb = np.arange(256)[None, :]
    db = qb // 64 - ki // 64
    mb = np.where((db >= 0) & (db <= 2), 0.0, NEG).astype(np.float32)
    slopes = (2.0 ** (-8.0 * (np.arange(8, dtype=np.float32) + 1.0) / 8)).astype(np.float32)
    absrel = np.abs(qb - ki).astype(np.float32)
    biasB = (-slopes[:, None, None] * absrel[None] + mb[None]).astype(np.float32)
    return maskA, biasB


def prep_shared(inputs, nlayers=NL):
    sh = {}
    for l in range(nlayers):
        sh[f"ws{l}"] = host_weight_stream(inputs, l)
    g = inputs["ln_g"].reshape(NL, 3, 16, 128)
    b = inputs["ln_b"].reshape(NL, 3, 16, 128)
    lnp = np.stack([g, b], axis=2)
    sh["lnp"] = np.ascontiguousarray(lnp.transpose(4, 0, 1, 2, 3).reshape(128, NL * 3 * 2 * 16), dtype=np.float32)
    r = np.arange(768) - 127
    idx = np.clip(r, -63, 128) + 63
    sh["extA"] = np.ascontiguousarray(inputs["rel_bias_a"][:, :, idx], dtype=np.float32)
    maskA, biasB = _const_masks()
    sh["maskA"] = maskA
    sh["biasB"] = biasB
    sh["sink"] = np.ascontiguousarray(inputs["sink_b"].reshape(NL * 8), dtype=np.float32)
    return sh


def prep_core(inputs, c, sh):
    b, q = c // 4, c % 4
    m = dict(sh)
    xp = inputs["x_prompt"][b]
    xin = np.zeros((NTILE, 128, 16, 512), np.float32)
    kmask = np.zeros((128, 16), np.float32)
    for ti in range(NTILE):
        s0 = q * 4096 + (ti - 4) * 512
        if s0 < 0:
            kmask[:, ti] = NEG
            continue
        xin[ti] = xp[s0:s0 + 512].reshape(512, 16, 128).transpose(2, 1, 0)
    m["xin"] = xin
    m["kmask"] = kmask
    m["xs"] = np.ascontiguousarray(inputs["x_sample"][c].reshape(16, 16, 128).transpose(2, 1, 0))
    m["memT"] = np.ascontiguousarray(inputs["mem_prompt"][b].reshape(256, 16, 128).transpose(2, 1, 0))
    m["cakT"] = np.ascontiguousarray(inputs["cache_a_k"][:, c].transpose(0, 3, 2, 1))
    m["cav"] = np.ascontiguousarray(inputs["cache_a_v"][:, c].reshape(NL, 512, 1024))
    m["cbkT"] = np.ascontiguousarray(inputs["cache_b_k"][:, c].transpose(0, 3, 2, 1))
    m["cbv"] = np.ascontiguousarray(inputs["cache_b_v"][:, c].reshape(NL, 128, 256))
    ck = inputs["cache_mem_k"][:, c].reshape(NL, 256, 4, 2, 128)
    m["cmkT"] = np.ascontiguousarray(ck.transpose(0, 4, 2, 3, 1).reshape(NL, 128, 8, 256))
    m["cmv"] = np.ascontiguousarray(inputs["cache_mem_v"][:, c].reshape(NL, 256, 1024))
    return m


def assemble(res):
    y_prompt = np.zeros((2, SEQ, DM), np.float32)
    y_sample = np.zeros((8, 16, DM), np.float32)
    a_k_p = np.zeros((NL, 2, 512, 8, 128), np.float32)
    a_v_p = np.zeros((NL, 2, 512, 8, 128), np.float32)
    b_k_p = np.zeros((NL, 2, 128, 2, 128), np.float32)
    b_v_p = np.zeros((NL, 2, 128, 2, 128), np.float32)
    m_k_p = np.zeros((NL, 2, 256, 4, 256), np.float32)
    m_v_p = np.zeros((NL, 2, 256, 4, 256), np.float32)
    a_k_s = np.zeros((NL, 8, 16, 8, 128), np.float32)
    a_v_s = np.zeros((NL, 8, 16, 8, 128), np.float32)
    b_k_s = np.zeros((NL, 8, 16, 2, 128), np.float32)
    b_v_s = np.zeros((NL, 8, 16, 2, 128), np.float32)
    for c in range(8):
        r = res[c]
        b, q = c // 4, c % 4
        yo = r["y_out"]
        y_prompt[b, q * 4096:(q + 1) * 4096] = yo.transpose(0, 3, 2, 1).reshape(4096, DM)
        y_sample[c] = r["ys_out"].transpose(2, 1, 0).reshape(16, DM)
        a_k_s[:, c] = r["aks"].transpose(0, 3, 2, 1)
        a_v_s[:, c] = r["avs"].reshape(NL, 16, 8, 128)
        b_k_s[:, c] = r["bks"].transpose(0, 3, 2, 1)
        b_v_s[:, c] = r["bvs"].reshape(NL, 16, 2, 128)
        if q == 3:
            a_k_p[:, b] = r["akp"].transpose(0, 3, 2, 1)
            a_v_p[:, b] = r["avp"].reshape(NL, 512, 8, 128)
            b_k_p[:, b] = r["bkp"].transpose(0, 3, 2, 1)
            b_v_p[:, b] = r["bvp"].reshape(NL, 128, 2, 128)
        if q == 0:
            mk = r["mkp"].reshape(NL, 128, 4, 2, 256)
            m_k_p[:, b] = mk.transpose(0, 4, 2, 3, 1).reshape(NL, 256, 4, 256)
            m_v_p[:, b] = r["mvp"].reshape(NL, 256, 4, 256)
    return (y_prompt, y_sample, a_k_p, a_v_p, b_k_p, b_v_p, m_k_p, m_v_p, a_k_s, a_v_s, b_k_s, b_v_s)


def kernel(**inputs):
    inputs = {k: np.asarray(v) for k, v in inputs.items()}
    nc = build_program()
    sh = prep_shared(inputs)
    in_maps = [prep_core(inputs, c, sh) for c in range(8)]
    res = run_bass_kernel_spmd(nc, in_maps, core_ids=list(range(8)))
    return assemble(res.results)
```
